# Optimizing a Trainium2 kernel written in Bass

```python
import math
import jax, jax.numpy as jnp
from jax import lax
import numpy as np

D_MODEL = 2048
BATCH = 4
SEQ = 4096
DEPTH = 4

N_MIXERS = 2
EPS = 1e-6
DA_HEADS = 8
DA_HEAD_DIM = 128
DA_V_DIM = 2 * DA_HEAD_DIM
DA_QK_WIDTH = DA_HEADS * 2 * DA_HEAD_DIM
DA_V_WIDTH = DA_HEADS * DA_V_DIM
ROPE_THETA = 500000.0
ROPE_DIM = DA_HEAD_DIM // 4
Q_BLOCK = 128
SSM_D_INNER = 2 * D_MODEL
SSM_HEAD_DIM = 64
SSM_HEADS = SSM_D_INNER // SSM_HEAD_DIM
SSM_GROUPS = 8
SSM_HPG = SSM_HEADS // SSM_GROUPS
SSM_D_STATE = 128
SSM_CONV = 4
SSM_CHUNK = 128
SSM_CONV_DIM = SSM_D_INNER + 2 * SSM_GROUPS * SSM_D_STATE
SSM_IN_DIM = SSM_D_INNER + SSM_CONV_DIM + SSM_HEADS
D_FF = 4 * D_MODEL
N_ATTN_LAYERS = (DEPTH + 1) // 2
N_SSM_LAYERS = DEPTH // 2

kernel_name = "hybrid_diffattn_mamba2_sqrelu"


def rms_norm(x, gain):
    xf = x.astype(jnp.float32)
    y = xf * lax.rsqrt(jnp.mean(xf * xf, axis=-1, keepdims=True) + EPS)
    return (y * gain.astype(jnp.float32)).astype(x.dtype)


def lambda_init(layer_idx):
    return 0.8 - 0.6 * math.exp(-0.3 * layer_idx)


def rope_tables(seq):
    inv = ROPE_THETA ** (-jnp.arange(0, ROPE_DIM, 2, dtype=jnp.float32) / ROPE_DIM)
    ang = jnp.arange(seq, dtype=jnp.float32)[:, None] * inv[None, :]
    return jnp.cos(ang)[None, :, None, None, :], jnp.sin(ang)[None, :, None, None, :]


def partial_rope(x, cos, sin):
    half = ROPE_DIM // 2
    cos = cos.astype(x.dtype)
    sin = sin.astype(x.dtype)
    x1 = x[..., :half]
    x2 = x[..., half:ROPE_DIM]
    return jnp.concatenate([x1 * cos - x2 * sin, x2 * cos + x1 * sin, x[..., ROPE_DIM:]], axis=-1)


def diff_attention(h, w_qkv, q_gain, k_gain, lam, subln_gain, w_o, lam_init):
    b, s, _ = h.shape
    qkv = h @ w_qkv
    q, k, v = jnp.split(qkv, [DA_QK_WIDTH, 2 * DA_QK_WIDTH], axis=-1)
    q = q.reshape(b, s, DA_HEADS, 2, DA_HEAD_DIM)
    k = k.reshape(b, s, DA_HEADS, 2, DA_HEAD_DIM)
    v = v.reshape(b, s, DA_HEADS, DA_V_DIM)
    cos, sin = rope_tables(s)
    q = partial_rope(rms_norm(q, q_gain), cos, sin)
    k = partial_rope(rms_norm(k, k_gain), cos, sin)
    lamf = lam.astype(jnp.float32)
    lam_full = jnp.exp(jnp.sum(lamf[0] * lamf[1])) - jnp.exp(jnp.sum(lamf[2] * lamf[3])) + lam_init
    scale = DA_HEAD_DIM ** -0.5
    nblk = s // Q_BLOCK
    qb = (q * scale).reshape(b, nblk, Q_BLOCK, DA_HEADS, 2, DA_HEAD_DIM).transpose(1, 0, 3, 4, 2, 5)
    kt = k.transpose(0, 2, 3, 1, 4)
    vt = v.transpose(0, 2, 1, 3)
    key_pos = jnp.arange(s)

    def block(args):
        qblk, i = args
        scores = jnp.einsum('bhcqd,bhckd->bhcqk', qblk, kt, preferred_element_type=jnp.float32)
        q_pos = i * Q_BLOCK + jnp.arange(Q_BLOCK)
        mask = key_pos[None, :] <= q_pos[:, None]
        p = jax.nn.softmax(jnp.where(mask, scores, -jnp.inf), axis=-1)
        a = p[:, :, 0] - lam_full * p[:, :, 1]
        return jnp.einsum('bhqk,bhkv->bhqv', a.astype(vt.dtype), vt)

    o = lax.map(block, (qb, jnp.arange(nblk)))
    o = o.transpose(1, 0, 3, 2, 4).reshape(b, s, DA_HEADS, DA_V_DIM)
    o = rms_norm(o, subln_gain) * (1.0 - lam_init)
    return o.reshape(b, s, DA_V_WIDTH) @ w_o


def ssd_chunked(x, dt, A, B, C):
    b, s = x.shape[:2]
    nc = s // SSM_CHUNK
    L = SSM_CHUNK

    def to_chunks(t):
        return jnp.moveaxis(t.reshape(b, nc, L, *t.shape[2:]), 1, 0)

    xc, dtc, Bc, Cc = to_chunks(x), to_chunks(dt), to_chunks(B), to_chunks(C)
    causal = jnp.tril(jnp.ones((L, L), dtype=bool))

    def step(state, inp):
        x_, dt_, B_, C_ = inp
        a = jnp.moveaxis(jnp.cumsum(dt_ * A, axis=1), 1, -1)
        xdt = x_ * dt_[..., None]
        seg = a[..., :, None] - a[..., None, :]
        decay = jnp.exp(jnp.where(causal, seg, -jnp.inf))
        cb = jnp.einsum('blgn,bsgn->bgls', C_, B_)
        y_diag = jnp.einsum('bgls,bghls,bsghp->blghp', cb, decay, xdt)
        y_off = jnp.einsum('blgn,bghpn,bghl->blghp', C_, state, jnp.exp(a))
        decay_to_end = jnp.exp(a[..., -1:] - a)
        new_state = state * jnp.exp(a[..., -1])[..., None, None] + jnp.einsum('bsgn,bghs,bsghp->bghpn', B_, decay_to_end, xdt)
        return new_state, y_diag + y_off

    state0 = jnp.zeros((b, SSM_GROUPS, SSM_HPG, SSM_HEAD_DIM, SSM_D_STATE), jnp.float32)
    _, y = lax.scan(step, state0, (xc, dtc, Bc, Cc))
    return jnp.moveaxis(y, 0, 1).reshape(b, s, SSM_GROUPS, SSM_HPG, SSM_HEAD_DIM)


def mamba2_mixer(h, w_in, conv_w, conv_b, dt_bias, a_log, d_skip, norm_gain, w_out):
    b, s, _ = h.shape
    zxbcdt = h @ w_in
    z, xbc, dt = jnp.split(zxbcdt, [SSM_D_INNER, SSM_D_INNER + SSM_CONV_DIM], axis=-1)
    xbc = lax.conv_general_dilated(xbc, conv_w[:, None, :].astype(xbc.dtype), window_strides=(1,),
                                   padding=[(SSM_CONV - 1, 0)], dimension_numbers=('NWC', 'WIO', 'NWC'),
                                   feature_group_count=SSM_CONV_DIM) + conv_b
    xbc = jax.nn.silu(xbc).astype(jnp.float32)
    xs, Bm, Cm = jnp.split(xbc, [SSM_D_INNER, SSM_D_INNER + SSM_GROUPS * SSM_D_STATE], axis=-1)
    dt = jax.nn.softplus(dt.astype(jnp.float32) + dt_bias.astype(jnp.float32))
    A = -jnp.exp(a_log.astype(jnp.float32))
    xs = xs.reshape(b, s, SSM_GROUPS, SSM_HPG, SSM_HEAD_DIM)
    y = ssd_chunked(xs, dt.reshape(b, s, SSM_GROUPS, SSM_HPG), A.reshape(SSM_GROUPS, SSM_HPG),
                    Bm.reshape(b, s, SSM_GROUPS, SSM_D_STATE), Cm.reshape(b, s, SSM_GROUPS, SSM_D_STATE))
    y = y + d_skip.astype(jnp.float32).reshape(SSM_GROUPS, SSM_HPG)[:, :, None] * xs
    y = y.reshape(b, s, SSM_D_INNER) * jax.nn.silu(z.astype(jnp.float32))
    y = rms_norm(y.reshape(b, s, SSM_GROUPS, SSM_D_INNER // SSM_GROUPS),
                 norm_gain.reshape(SSM_GROUPS, SSM_D_INNER // SSM_GROUPS)).reshape(b, s, SSM_D_INNER)
    return y.astype(h.dtype) @ w_out


def sq_relu_mlp(h, w1, w2):
    return jnp.square(jax.nn.relu(h @ w1)) @ w2


def setup_inputs(seed: int = 0) -> dict:
    key = jax.random.key(seed)
    ks = jax.random.split(key, 20)
    f32 = jnp.float32
    nrm = lambda k, shape, scale: jax.random.normal(k, shape, f32) * scale
    dt0 = jnp.exp(jax.random.uniform(ks[11], (N_SSM_LAYERS, SSM_HEADS), f32, math.log(1e-3), math.log(1e-1)))
    return {
        "x": nrm(ks[0], (BATCH, SEQ, D_MODEL), 1.0),
        "mixer_norm": 1.0 + nrm(ks[1], (DEPTH, D_MODEL), 0.02),
        "mlp_norm": 1.0 + nrm(ks[2], (DEPTH, D_MODEL), 0.02),
        "attn_w_qkv": nrm(ks[3], (N_ATTN_LAYERS, D_MODEL, 2 * DA_QK_WIDTH + DA_V_WIDTH), D_MODEL ** -0.5),
        "attn_q_norm": 1.0 + nrm(ks[4], (N_ATTN_LAYERS, DA_HEAD_DIM), 0.02),
        "attn_k_norm": 1.0 + nrm(ks[5], (N_ATTN_LAYERS, DA_HEAD_DIM), 0.02),
        "attn_lambda": nrm(ks[6], (N_ATTN_LAYERS, 4, DA_HEAD_DIM), 0.1),
        "attn_subln": 1.0 + nrm(ks[7], (N_ATTN_LAYERS, DA_V_DIM), 0.02),
        "attn_w_o": nrm(ks[8], (N_ATTN_LAYERS, DA_V_WIDTH, D_MODEL), DA_V_WIDTH ** -0.5),
        "ssm_w_in": nrm(ks[9], (N_SSM_LAYERS, D_MODEL, SSM_IN_DIM), D_MODEL ** -0.5),
        "ssm_conv_w": nrm(ks[10], (N_SSM_LAYERS, SSM_CONV, SSM_CONV_DIM), SSM_CONV ** -0.5),
        "ssm_conv_b": nrm(ks[12], (N_SSM_LAYERS, SSM_CONV_DIM), 0.02),
        "ssm_dt_bias": dt0 + jnp.log(-jnp.expm1(-dt0)),
        "ssm_a_log": jnp.log(jax.random.uniform(ks[13], (N_SSM_LAYERS, SSM_HEADS), f32, 1.0, 16.0)),
        "ssm_d": 1.0 + nrm(ks[14], (N_SSM_LAYERS, SSM_HEADS), 0.02),
        "ssm_norm": 1.0 + nrm(ks[15], (N_SSM_LAYERS, SSM_D_INNER), 0.02),
        "ssm_w_out": nrm(ks[16], (N_SSM_LAYERS, SSM_D_INNER, D_MODEL), SSM_D_INNER ** -0.5),
        "mlp_w1": nrm(ks[17], (DEPTH, D_MODEL, D_FF), D_MODEL ** -0.5),
        "mlp_w2": nrm(ks[18], (DEPTH, D_FF, D_MODEL), D_FF ** -0.5),
    }


def reference(x, mixer_norm, mlp_norm, attn_w_qkv, attn_q_norm, attn_k_norm, attn_lambda, attn_subln, attn_w_o,
              ssm_w_in, ssm_conv_w, ssm_conv_b, ssm_dt_bias, ssm_a_log, ssm_d, ssm_norm, ssm_w_out,
              mlp_w1, mlp_w2):
    h = x
    for i in range(DEPTH):
        hn = rms_norm(h, mixer_norm[i])
        j = i // N_MIXERS
        if i % N_MIXERS == 0:
            mix = diff_attention(hn, attn_w_qkv[j], attn_q_norm[j], attn_k_norm[j], attn_lambda[j],
                                 attn_subln[j], attn_w_o[j], lambda_init(i))
        else:
            mix = mamba2_mixer(hn, ssm_w_in[j], ssm_conv_w[j], ssm_conv_b[j], ssm_dt_bias[j], ssm_a_log[j],
                               ssm_d[j], ssm_norm[j], ssm_w_out[j])
        h = h + mix.astype(h.dtype)
        h = h + sq_relu_mlp(rms_norm(h, mlp_norm[i]), mlp_w1[i], mlp_w2[i]).astype(h.dtype)
    return h
```

```python
import math
from contextlib import ExitStack
import numpy as np
import concourse.bass as bass
import concourse.mybir as mybir
from concourse.bass_utils import run_bass_kernel_spmd

F32 = mybir.dt.float32
BF16 = mybir.dt.bfloat16
ALU = mybir.AluOpType
AF = mybir.ActivationFunctionType
AX = mybir.AxisListType

EPS = 1e-6
GROUPS = [[0, 1], [2, 3], [4, 5], [6, 7]]
CE = ("pe", "act", "dve", "pool")


def lambda_init(i):
    return 0.8 - 0.6 * math.exp(-0.3 * i)


SSM_STOP = [0]


class _Stop(Exception):
    pass


def chk(n):
    if SSM_STOP[0] == n:
        raise _Stop()


class Buf:
    __slots__ = ("w", "r")

    def __init__(self):
        self.w = None
        self.r = {}


class T:
    def __init__(self, t, name, psum=False):
        self.t = t
        self.name = name
        self.b = Buf()
        self.psum = psum

    def __getitem__(self, k):
        return self.t[k]


class Prog:
    ENG = ("pe", "act", "dve", "pool", "sp")

    def __init__(self, nc, stack):
        self.nc = nc
        self.stack = stack
        self.sem = {}
        self.cnt = {}
        self.q = {e: [] for e in self.ENG}
        self.known = {e: {} for e in self.ENG}
        for e in CE:
            self._sem(e)

    def _sem(self, key):
        if key not in self.sem:
            self.sem[key] = self.stack.enter_context(self.nc.semaphore("s_" + key))
            self.cnt[key] = 0
        return self.sem[key]

    def _need(self, eng, key, val, waits):
        if key not in CE:
            val = self.cnt[key]
        if key == "pe" and eng == "pe":
            return
        if self.known[eng].get(key, 0) < val:
            self.known[eng][key] = val
            waits[key] = val

    def op(self, eng, fn, r=(), w=(), dma=None, inc=16):
        w = list(w) + [t for t in r if t.psum and t not in w]
        r = [t for t in r if not t.psum]
        waits = {}
        for t in r:
            if t.b.w is not None:
                self._need(eng, t.b.w[0], t.b.w[1], waits)
        for t in w:
            if t.b.w is not None:
                self._need(eng, t.b.w[0], t.b.w[1], waits)
            for k, v in t.b.r.items():
                self._need(eng, k, v, waits)
        if dma is None:
            self.cnt[eng] += 1
            ev = (eng, self.cnt[eng])
        else:
            self._sem(dma)
            self.cnt[dma] += inc
            ev = (dma, self.cnt[dma])
        for t in w:
            t.b.w = ev
            t.b.r = {}
        for t in r:
            if t not in w:
                t.b.r[ev[0]] = ev[1]
        self.q[eng].append((list(waits.items()), fn, ev, dma))

    def dma(self, q, out_ap, in_ap, key, r=(), w=()):
        def fn(e, sem):
            e.dma_start(out=out_ap, in_=in_ap).then_inc(sem, 16)
        self.op(q, fn, r=r, w=w, dma=key, inc=16)

    def barrier(self):
        for eng in self.ENG:
            waits = []
            for key, c in self.cnt.items():
                if c > 0 and self.known[eng].get(key, 0) < c and not (key == "pe" and eng == "pe"):
                    self.known[eng][key] = c
                    waits.append((key, c))
            self.q[eng].append((waits, None, None, None))

    def flush(self):
        nc = self.nc
        q = self.q
        self.q = {e: [] for e in self.ENG}
        sem = self.sem

        def mk(name):
            def body(e):
                for waits, fn, ev, dma in q[name]:
                    for key, val in waits:
                        e.wait_ge(sem[key], val)
                    if fn is None:
                        continue
                    if dma is None:
                        fn(e).then_inc(sem[ev[0]], 1)
                    else:
                        fn(e, sem[dma])
            return body

        with nc.Block() as block:
            block.tensor(mk("pe"))
            block.scalar(mk("act"))
            block.vector(mk("dve"))
            block.gpsimd(mk("pool"))
            block.sync(mk("sp"))

    def end_phase(self):
        self.barrier()
        self.flush()

    def sb(self, st, name, shape, dt):
        self.uid = getattr(self, "uid", 0) + 1
        return T(st.enter_context(self.nc.sbuf_tensor(f"{name}_{self.uid}", shape, dt)), name)

    def ps(self, st, name, shape, dt):
        self.uid = getattr(self, "uid", 0) + 1
        return T(st.enter_context(self.nc.psum_tensor(f"{name}_{self.uid}", shape, dt)), name, psum=True)


class Rot:
    def __init__(self, tiles):
        self.tiles = tiles
        self.i = 0

    def next(self):
        t = self.tiles[self.i % len(self.tiles)]
        self.i += 1
        return t


def build(n_layers=4, debug=False, mode="full"):
    nc = bass.Bass("TRN2", target_bir_lowering=False)
    dt_in = lambda n, s: nc.dram_tensor(n, s, F32, kind="ExternalInput")
    xT = dt_in("xT", [16, 128, 2048])
    selv_d = dt_in("selv", [128, 2])
    mixg_d = dt_in("mixg", [128, 64])
    mlpg_d = dt_in("mlpg", [128, 64])
    small = (mode != "full")
    wqkv = dt_in("wqkv", [1, 128, 128] if small else [2, 2048, 3072])
    qg_d = dt_in("qg", [2, 128])
    kg_d = dt_in("kg", [2, 128])
    lam_d = dt_in("lam", [2, 512])
    subln_d = dt_in("subln", [2, 256])
    wo = dt_in("wo", [1, 128, 128] if small else [2, 2048, 2048])
    win = dt_in("win", [2, 2048, 5376])
    convw_d = dt_in("convw", [2, 128, 96])
    convb_d = dt_in("convb", [2, 128, 24])
    dtb_d = dt_in("dtb", [2, 32])
    alog_d = dt_in("alog", [2, 32])
    dsk_d = dt_in("dsk", [2, 32])
    ngain_d = dt_in("ngain", [2, 2048])
    wout = dt_in("wout", [1, 128, 128] if small else [2, 4096, 2048])
    w1 = dt_in("w1", [1, 128, 128] if small else [n_layers, 2048, 8192])
    w2 = dt_in("w2", [1, 128, 128] if small else [n_layers, 8192, 2048])
    cos_d = dt_in("cos", [128, 512])
    sin_d = dt_in("sin", [128, 512])
    outT = nc.dram_tensor("outT", [16, 128, 2048], F32, kind="ExternalOutput")

    if debug:
        dbg_hn = nc.dram_tensor("dbg_hn", [4, 1024, 2048], BF16, kind="ExternalOutput")
        dbg_y = nc.dram_tensor("dbg_y", [8, 512, 4096], BF16, kind="ExternalOutput")
        dbg_mix = nc.dram_tensor("dbg_mix", [16, 128, 2048], F32, kind="ExternalOutput")
    res = nc.dram_tensor("res", [16, 128, 2048], F32)
    hn_own = nc.dram_tensor("hn_own", [4, 512, 2048], BF16)
    if mode == "ssm":
        hn_full = nc.dram_tensor("hn_in", [4, 1024, 2048], BF16, kind="ExternalInput")
        y_own = nc.dram_tensor("y_out", [8, 256, 4096], BF16, kind="ExternalOutput")
    else:
        hn_full = nc.dram_tensor("hn_full", [4, 1024, 2048], BF16)
        y_own = nc.dram_tensor("y_own", [8, 256, 4096], BF16)
    y_full = nc.dram_tensor("y_full", [8, 512, 4096], BF16)
    D_hn_own, D_hn_full, D_y_own, D_y_full = (T(None, n) for n in ("hn_own", "hn_full", "y_own", "y_full"))
    D_res = [T(None, f"res{i}") for i in range(2)]
    D_out = T(None, "out")

    with ExitStack() as gs:
        P = Prog(nc, gs)
        ident_bf = P.sb(gs, "ident_bf", [128, 128], BF16)
        ones_f = P.sb(gs, "ones_f", [128, 128], F32)
        tri_f = P.sb(gs, "tri_f", [128, 128], F32)
        tri_bf = P.sb(gs, "tri_bf", [128, 128], BF16)
        U_f = P.sb(gs, "U_f", [128, 128], F32)
        selv = P.sb(gs, "selv_sb", [128, 2], F32)
        mixg = P.sb(gs, "mixg_sb", [128, 64], F32)
        mlpg = P.sb(gs, "mlpg_sb", [128, 64], F32)

        P.op("pool", lambda e: e.memset(ones_f[:, :], 1.0), w=[ones_f])
        P.op("pool", lambda e: e.affine_select(out=tri_f[:, :], in_=ones_f[:, :], pattern=[[1, 128]],
                                                 compare_op=ALU.is_ge, fill=0.0, base=0, channel_multiplier=-1),
             r=[ones_f], w=[tri_f])
        P.op("pool", lambda e: e.affine_select(out=U_f[:, :], in_=ones_f[:, :], pattern=[[-1, 128]],
                                                 compare_op=ALU.is_ge, fill=0.0, base=-1, channel_multiplier=1),
             r=[ones_f], w=[U_f])
        P.op("pool", lambda e: e.affine_select(out=ident_bf[:, :], in_=ones_f[:, :], pattern=[[1, 128]],
                                                 compare_op=ALU.is_equal, fill=0.0, base=0, channel_multiplier=-1),
             r=[ones_f], w=[ident_bf])
        P.op("pool", lambda e: e.tensor_copy(out=tri_bf[:, :], in_=tri_f[:, :]), r=[tri_f], w=[tri_bf])
        P.dma("sp", selv[:, :], selv_d.ap(), "const", w=[selv])
        P.dma("sp", mixg[:, :], mixg_d.ap(), "const", w=[mixg])
        P.dma("sp", mlpg[:, :], mlpg_d.ap(), "const", w=[mlpg])
        P.end_phase()

        def load_w(slot, W2d, k0, n0, ncols):
            src = W2d[k0 * 128:(k0 + 16) * 128, n0:n0 + ncols].rearrange("(c p) n -> p c n", p=128)
            P.dma("pool", slot[:, :, 0:ncols], src, "d_" + slot.name, w=[slot])

        def allgather(src_t, dst_t, n, Dsrc, Ddst):
            for c in range(n):
                def fn(e, sem, c=c):
                    e.collective_compute("AllGather", ALU.bypass, replica_groups=GROUPS,
                                         ins=[src_t.ap()[c]], outs=[dst_t.ap()[c]]).then_inc(sem, 1)
                P.op("pool", fn, r=[Dsrc], w=[Ddst], dma="cc", inc=1)

        def phase_A(layer):
            src = xT if layer == 0 else res
            with ExitStack() as st:
                rt = [P.sb(st, f"A_rt{i}", [128, 16, 512], F32) for i in range(2)]
                sq = P.sb(st, "A_sq", [128, 16, 512], F32)
                rs = P.sb(st, "A_rs", [128, 512], F32)
                ri = P.sb(st, "A_ri", [128, 512], F32)
                hn = [P.sb(st, f"A_hn{i}", [128, 16, 512], BF16) for i in range(2)]
                ps = P.ps(st, "A_ps", [128, 512], F32)
                hview = hn_own.ap().rearrange("a (j p) t -> p (a j) t", p=128)
                for tt in range(4):
                    R = rt[tt % 2]
                    H = hn[tt % 2]
                    P.dma("sp", R[:, :, :], src.ap()[:, :, tt * 512:(tt + 1) * 512].rearrange("c p t -> p c t"),
                          "d_" + R.name, r=[D_res[tt // 2]], w=[R])
                    P.op("act", lambda e, R=R: e.activation(out=sq[:, :, :], in_=R[:, :, :], func=AF.Square),
                         r=[R], w=[sq])

                    def mm(e):
                        for c in range(16):
                            last = e.matmul(ps[:, :], lhsT=ones_f[:, :], rhs=sq[:, c, :], start=(c == 0), stop=(c == 15))
                        return last
                    P.op("pe", mm, r=[sq, ones_f], w=[ps])
                    P.op("act", lambda e: e.activation(out=rs[:, :], in_=ps[:, :], func=AF.Sqrt, bias=EPS_AP[:, 0:1],
                                                       scale=1.0 / 2048), r=[ps, EPS_T], w=[rs])
                    P.op("dve", lambda e: e.reciprocal(out=ri[:, :], in_=rs[:, :]), r=[rs], w=[ri])

                    def nrm(e, R=R, H=H):
                        for c in range(16):
                            last = e.scalar_tensor_tensor(out=H[:, c, :], in0=R[:, c, :],
                                                          scalar=mixg[:, layer * 16 + c:layer * 16 + c + 1],
                                                          in1=ri[:, :], op0=ALU.mult, op1=ALU.mult)
                        return last
                    P.op("dve", nrm, r=[R, ri, mixg], w=[H])
                    P.dma("sp", hview[:, :, tt * 512:(tt + 1) * 512], H[:, :, :], "d_" + H.name, r=[H], w=[D_hn_own])
                allgather(hn_own, hn_full, 4, D_hn_own, D_hn_full)
                if debug and layer == n_layers - 1:
                    P.dma("sp", dbg_hn.ap(), hn_full.ap(), "dbg", r=[D_hn_full])
                P.end_phase()

        def load_hn(HN, g):
            r, t0 = g // 4, (g % 4) * 512
            def fn(e, sem):
                for c in range(4):
                    e.dma_start(out=HN[:, c, :, :],
                                in_=hn_full.ap()[c, r * 512:(r + 1) * 512, t0:t0 + 512].rearrange("(j p) t -> p j t", p=128)
                                ).then_inc(sem, 16)
            P.op("sp", fn, r=[D_hn_full], w=[HN], dma="d_" + HN.name, inc=64)

        def phase_B_attn(layer):
            j = layer // 2
            linit = lambda_init(layer)
            with ExitStack() as st:
                QKT = P.sb(st, "QKT", [128, 4, 4096], BF16)
                Vaug = P.sb(st, "Vaug", [128, 32, 258], BF16)
                HNr = Rot([P.sb(st, f"HN{i}", [128, 4, 4, 512], BF16) for i in range(2)])
                wA = [P.sb(st, f"wA{i}", [128, 16, 512], BF16) for i in range(2)]
                wB = [P.sb(st, f"wB{i}", [128, 16, 256], BF16) for i in range(2)]
                G4 = P.sb(st, "G4", [128, 4, 128], F32)
                SG = P.sb(st, "SG", [128, 256], F32)
                COS = P.sb(st, "COS", [128, 32, 16], F32)
                SIN = P.sb(st, "SIN", [128, 32, 16], F32)
                row = P.sb(st, "row", [1, 1024], F32)
                sc = P.sb(st, "sc", [128, 8], F32)
                QK4r = Rot([P.sb(st, f"QK4_{i}", [128, 4, 512], F32) for i in range(2)])
                W4 = P.sb(st, "W4", [128, 4, 512], F32)
                s16 = P.sb(st, "s16", [128, 64], F32)
                rp = [P.sb(st, f"rp{i}", [128, 4, 4, 16], F32) for i in range(4)]
                QNb4r = Rot([P.sb(st, f"QNb4_{i}", [128, 4, 4, 128], BF16) for i in range(2)])
                pT = Rot([P.sb(st, f"pT{i}", [128, 512], BF16) for i in range(3)])
                O1n = P.sb(st, "O1n", [128, 4, 256], F32)
                ob = P.sb(st, "ob", [128, 256], F32)
                osq = P.sb(st, "osq", [128, 256], F32)
                onb = P.sb(st, "onb", [128, 256], BF16)
                e1 = P.sb(st, "e1", [128, 8], F32)
                oT = Rot([P.sb(st, f"oT{i}", [128, 2, 512], BF16) for i in range(2)])
                O = [P.ps(st, f"O{i}", [128, 512], F32) for i in range(4)]
                psr = Rot([P.ps(st, f"psr{i}", [128, 512], F32) for i in range(3)])
                psb = P.ps(st, "psb", [128, 8, 128], BF16)
                projrot = Rot(psr.tiles + O)

                bc = lambda d, n: d.ap()[j, :].partition_broadcast(128)
                P.dma("sp", G4[:, 0, :], bc(qg_d, 128), "const", w=[G4])
                P.dma("sp", G4[:, 1, :], bc(qg_d, 128), "const", w=[G4])
                P.dma("sp", G4[:, 2, :], bc(kg_d, 128), "const", w=[G4])
                P.dma("sp", G4[:, 3, :], bc(kg_d, 128), "const", w=[G4])
                P.dma("sp", SG[:, :], bc(subln_d, 256), "const", w=[SG])
                P.dma("sp", COS[:, :, :], cos_d.ap().rearrange("p (b f) -> p b f", f=16), "const", w=[COS])
                P.dma("sp", SIN[:, :, :], sin_d.ap().rearrange("p (b f) -> p b f", f=16), "const", w=[SIN])
                P.dma("sp", row[:, 0:512], lam_d.ap()[j:j + 1, :], "const", w=[row])
                P.dma("sp", row[:, 512:640], qg_d.ap()[j:j + 1, :], "const", w=[row])
                P.dma("sp", row[:, 640:768], kg_d.ap()[j:j + 1, :], "const", w=[row])
                P.op("act", lambda e: e.mul(out=G4[:, 0:2, :], in_=G4[:, 0:2, :], mul=128.0 ** -0.5), r=[G4], w=[G4])
                P.op("act", lambda e: e.mul(out=SG[:, :], in_=SG[:, :], mul=1.0 - linit), r=[SG], w=[SG])
                P.op("pool", lambda e: e.memset(Vaug[:, :, 256:258], 1.0), w=[Vaug])
                lv = row[:, 0:512].rearrange("p (a b d) -> p a b d", a=2, b=2)
                P.op("dve", lambda e: e.tensor_tensor(out=row[:, 768:1024].rearrange("p (a d) -> p a d", a=2),
                                                      in0=lv[:, :, 0, :], in1=lv[:, :, 1, :], op=ALU.mult), r=[row], w=[row])
                P.op("dve", lambda e: e.tensor_reduce(out=row[:, 0:2], in_=row[:, 768:1024].rearrange("p (a d) -> p a d", a=2),
                                                      axis=AX.X, op=ALU.add), r=[row], w=[row])
                P.op("act", lambda e: e.activation(out=row[:, 2:4], in_=row[:, 0:2], func=AF.Exp), r=[row], w=[row])
                P.op("dve", lambda e: e.tensor_tensor(out=row[:, 4:5], in0=row[:, 3:4], in1=row[:, 2:3], op=ALU.subtract),
                     r=[row], w=[row])
                P.op("dve", lambda e: e.tensor_scalar(out=row[:, 8:9], in0=row[:, 4:5], scalar1=-linit, scalar2=None,
                                                      op0=ALU.add), r=[row], w=[row])
                P.op("dve", lambda e: e.tensor_reduce(out=row[:, 5:7], in_=row[:, 512:768].rearrange("p (a d) -> p a d", a=2),
                                                      axis=AX.X, op=ALU.max, apply_absolute_value=True), r=[row], w=[row])
                P.op("dve", lambda e: e.tensor_tensor(out=row[:, 7:8], in0=row[:, 5:6], in1=row[:, 6:7], op=ALU.mult),
                     r=[row], w=[row])
                P.op("dve", lambda e: e.tensor_scalar(out=row[:, 9:10], in0=row[:, 7:8], scalar1=-(128.0 ** 0.5), scalar2=None,
                                                      op0=ALU.mult), r=[row], w=[row])
                pc = psr.next()
                P.op("pe", lambda e: e.matmul(pc[:, 0:2], lhsT=ones_f[0:1, :], rhs=row[:, 8:10], start=True, stop=True),
                     r=[row, ones_f], w=[pc])
                P.op("act", lambda e: e.copy(out=sc[:, 0:2], in_=pc[:, 0:2]), r=[pc], w=[sc])

                Wl = wqkv.ap()[j]
                for hd in range(4):
                    WA, WB = wA[hd % 2], wB[hd % 2]
                    load_w(WA, Wl, 0, hd * 768, 512)
                    load_w(WB, Wl, 0, hd * 768 + 512, 256)
                    def emit_tr(g, QN):
                        for rnd in range(2):
                            def tr(e, rnd=rnd, QN=QN):
                                for bl in range(2):
                                    for cc in range(4):
                                        last = e.transpose(out=psb[:, bl * 4 + cc, :], in_=QN[:, rnd * 2 + bl, cc, :], identity=ident_bf[:, :])
                                return last
                            P.op("pe", tr, r=[QN, ident_bf], w=[psb])
                            tok0 = (g * 4 + rnd * 2) * 128
                            P.op("act", lambda e, tok0=tok0: e.copy(
                                out=QKT[:, :, tok0:tok0 + 256].rearrange("p c (b t) -> p b c t", b=2),
                                in_=psb[:, :, :].rearrange("p (b c) t -> p b c t", b=2)), r=[psb], w=[QKT])
                    prev_tr = None
                    for g in range(8):
                        HN = HNr.next()
                        load_hn(HN, g)
                        QK4 = QK4r.next()
                        QNb4 = QNb4r.next()
                        for t4 in range(4):
                            tb = g * 4 + t4
                            pq, pv = projrot.next(), projrot.next()

                            def mmq(e, pq=pq, t4=t4, WA=WA, HN=HN):
                                for dc in range(16):
                                    last = e.matmul(pq[:, :], lhsT=HN[:, dc // 4, dc % 4, t4 * 128:(t4 + 1) * 128],
                                                    rhs=WA[:, dc, :], start=(dc == 0), stop=(dc == 15))
                                return last
                            P.op("pe", mmq, r=[HN, WA], w=[pq])

                            def mmv(e, pv=pv, t4=t4, WB=WB, HN=HN):
                                for dc in range(16):
                                    last = e.matmul(pv[:, 0:256], lhsT=HN[:, dc // 4, dc % 4, t4 * 128:(t4 + 1) * 128],
                                                    rhs=WB[:, dc, :], start=(dc == 0), stop=(dc == 15))
                                return last
                            P.op("pe", mmv, r=[HN, WB], w=[pv])
                            P.op("act", lambda e, pv=pv, tb=tb: e.copy(out=Vaug[:, tb, 0:256], in_=pv[:, 0:256]), r=[pv], w=[Vaug])
                            P.op("act", lambda e, pq=pq, t4=t4, QK4=QK4: e.copy(out=QK4[:, t4, :], in_=pq[:, :]), r=[pq], w=[QK4])
                        if prev_tr is not None:
                            emit_tr(*prev_tr)
                        f16 = lambda T_: T_[:, :, :].rearrange("p b (a d) -> p (b a) d", d=128)
                        f4 = lambda T_: T_[:, :, :].rearrange("p b (a d) -> p b a d", d=128)
                        P.op("act", lambda e, QK4=QK4: e.activation(out=W4[:, :, :], in_=QK4[:, :, :], func=AF.Square), r=[QK4], w=[W4])
                        P.op("dve", lambda e: e.tensor_reduce(out=s16[:, 0:16], in_=f16(W4), axis=AX.X, op=ALU.add), r=[W4], w=[s16])
                        P.op("dve", lambda e: e.tensor_scalar(out=s16[:, 16:32], in0=s16[:, 0:16], scalar1=1.0 / 128, scalar2=EPS,
                                                              op0=ALU.mult, op1=ALU.add), r=[s16], w=[s16])
                        P.op("act", lambda e: e.activation(out=s16[:, 32:48], in_=s16[:, 16:32], func=AF.Sqrt), r=[s16], w=[s16])
                        P.op("dve", lambda e: e.reciprocal(out=s16[:, 48:64], in_=s16[:, 32:48]), r=[s16], w=[s16])
                        P.op("dve", lambda e, QK4=QK4: e.tensor_tensor(out=f16(W4), in0=f16(QK4),
                                                                       in1=s16[:, 48:64].unsqueeze(2).broadcast_to([128, 16, 128]), op=ALU.mult),
                             r=[QK4, s16], w=[W4])
                        P.op("dve", lambda e: e.tensor_tensor(out=f4(W4), in0=f4(W4),
                                                              in1=G4[:, :, :].unsqueeze(1).broadcast_to([128, 4, 4, 128]), op=ALU.mult),
                             r=[W4, G4], w=[W4])
                        cb = lambda g=g: COS[:, g * 4:(g + 1) * 4, :].unsqueeze(2).broadcast_to([128, 4, 4, 16])
                        sb_ = lambda g=g: SIN[:, g * 4:(g + 1) * 4, :].unsqueeze(2).broadcast_to([128, 4, 4, 16])
                        P.op("dve", lambda e, cb=cb: e.tensor_tensor(out=rp[0][:, :, :, :], in0=f4(W4)[:, :, :, 0:16], in1=cb(), op=ALU.mult),
                             r=[W4, COS], w=[rp[0]])
                        P.op("dve", lambda e, sb_=sb_: e.tensor_tensor(out=rp[1][:, :, :, :], in0=f4(W4)[:, :, :, 16:32], in1=sb_(), op=ALU.mult),
                             r=[W4, SIN], w=[rp[1]])
                        P.op("dve", lambda e, cb=cb: e.tensor_tensor(out=rp[2][:, :, :, :], in0=f4(W4)[:, :, :, 16:32], in1=cb(), op=ALU.mult),
                             r=[W4, COS], w=[rp[2]])
                        P.op("dve", lambda e, sb_=sb_: e.tensor_tensor(out=rp[3][:, :, :, :], in0=f4(W4)[:, :, :, 0:16], in1=sb_(), op=ALU.mult),
                             r=[W4, SIN], w=[rp[3]])
                        P.op("dve", lambda e, QNb4=QNb4: e.tensor_tensor(out=QNb4[:, :, :, 0:16], in0=rp[0][:, :, :, :], in1=rp[1][:, :, :, :], op=ALU.subtract),
                             r=[rp[0], rp[1]], w=[QNb4])
                        P.op("dve", lambda e, QNb4=QNb4: e.tensor_tensor(out=QNb4[:, :, :, 16:32], in0=rp[2][:, :, :, :], in1=rp[3][:, :, :, :], op=ALU.add),
                             r=[rp[2], rp[3]], w=[QNb4])
                        P.op("act", lambda e, QNb4=QNb4: e.copy(out=QNb4[:, :, :, 32:128], in_=f4(W4)[:, :, :, 32:128]), r=[W4], w=[QNb4])
                        prev_tr = (g, QNb4)
                    emit_tr(*prev_tr)
                    for jq in range(8):
                        q0 = jq * 512
                        OT = oT.next()
                        for c in range(2):
                            nkb = 4 * jq + 4
                            sps = {}

                            def issue_S(kb, c=c, jq=jq, q0=q0):
                                r0 = max(0, kb - 4 * jq)
                                off = r0 * 128
                                sp_ = psr.next()
                                P.op("pe", lambda e, sp_=sp_, kb=kb, c=c, off=off, q0=q0: e.matmul(
                                    sp_[:, off:512], lhsT=QKT[:, 2 + c, kb * 128:(kb + 1) * 128],
                                    rhs=QKT[:, c, q0 + off:q0 + 512], start=True, stop=True), r=[QKT], w=[sp_])
                                sps[kb] = (sp_, off, r0)
                            for kb in range(min(2, nkb)):
                                issue_S(kb)
                            for kb in range(nkb):
                                if kb + 2 < nkb:
                                    issue_S(kb + 2)
                                sp_, off, r0 = sps.pop(kb)
                                pt = pT.next()
                                P.op("act", lambda e, sp_=sp_, pt=pt, off=off: e.activation(
                                    out=pt[:, off:512], in_=sp_[:, off:512], func=AF.Exp, bias=sc[:, 1:2], scale=1.0),
                                    r=[sp_, sc], w=[pt])
                                if kb >= 4 * jq:
                                    P.op("dve", lambda e, pt=pt, off=off: e.tensor_tensor(
                                        out=pt[:, off:off + 128], in0=pt[:, off:off + 128], in1=tri_bf[:, :], op=ALU.mult),
                                        r=[pt, tri_bf], w=[pt])

                                def pv_(e, pt=pt, kb=kb, r0=r0, jq=jq):
                                    for qb in range(r0, 4):
                                        last = e.matmul(O[qb][:, 0:257], lhsT=pt[:, qb * 128:(qb + 1) * 128],
                                                        rhs=Vaug[:, kb, 0:257], start=(kb == 0), stop=(kb == 4 * jq + qb))
                                    return last
                                P.op("pe", pv_, r=[pt, Vaug], w=[O[qb] for qb in range(r0, 4)])
                            for qb in range(4):
                                Oq = O[qb]
                                P.op("dve", lambda e, Oq=Oq: e.reciprocal(out=e1[:, 0:1], in_=Oq[:, 256:257]), r=[Oq], w=[e1])
                                if c == 0:
                                    P.op("dve", lambda e, Oq=Oq, qb=qb: e.tensor_scalar(
                                        out=O1n[:, qb, :], in0=Oq[:, 0:256], scalar1=e1[:, 0:1], scalar2=None, op0=ALU.mult),
                                        r=[Oq, e1], w=[O1n])
                                else:
                                    P.op("dve", lambda e: e.tensor_tensor(out=e1[:, 1:2], in0=e1[:, 0:1], in1=sc[:, 0:1], op=ALU.mult),
                                         r=[e1, sc], w=[e1])
                                    P.op("dve", lambda e, Oq=Oq, qb=qb: e.scalar_tensor_tensor(
                                        out=ob[:, :], in0=Oq[:, 0:256], scalar=e1[:, 1:2], in1=O1n[:, qb, :],
                                        op0=ALU.mult, op1=ALU.add), r=[Oq, e1, O1n], w=[ob])
                                    P.op("act", lambda e: e.activation(out=osq[:, :], in_=ob[:, :], func=AF.Square,
                                                                       accum_out=e1[:, 2:3]), r=[ob], w=[osq, e1])
                                    P.op("dve", lambda e: e.tensor_scalar(out=e1[:, 3:4], in0=e1[:, 2:3], scalar1=1.0 / 256,
                                                                          scalar2=EPS, op0=ALU.mult, op1=ALU.add), r=[e1], w=[e1])
                                    P.op("act", lambda e: e.activation(out=e1[:, 4:5], in_=e1[:, 3:4], func=AF.Sqrt), r=[e1], w=[e1])
                                    P.op("dve", lambda e: e.reciprocal(out=e1[:, 5:6], in_=e1[:, 4:5]), r=[e1], w=[e1])
                                    P.op("dve", lambda e: e.scalar_tensor_tensor(
                                        out=onb[:, :], in0=ob[:, :], scalar=e1[:, 5:6], in1=SG[:, :], op0=ALU.mult, op1=ALU.mult),
                                        r=[ob, e1, SG], w=[onb])

                                    def tr2(e):
                                        for i in range(2):
                                            last = e.transpose(out=psb[:, i, :], in_=onb[:, i * 128:(i + 1) * 128], identity=ident_bf[:, :])
                                        return last
                                    P.op("pe", tr2, r=[onb, ident_bf], w=[psb])
                                    P.op("act", lambda e, OT=OT, qb=qb: e.copy(out=OT[:, :, qb * 128:(qb + 1) * 128], in_=psb[:, 0:2, :]),
                                         r=[psb], w=[OT])
                        yv = y_own.ap().rearrange("a b t -> (a b) t")[hd * 256:(hd + 1) * 256, q0:q0 + 512].rearrange("(i p) t -> p i t", p=128)
                        P.dma("sp", yv, OT[:, :, :], "d_" + OT.name, r=[OT], w=[D_y_own])
                allgather(y_own, y_full, 4, D_y_own, D_y_full)
                if debug and layer == n_layers - 1:
                    P.dma("sp", dbg_y.ap(), y_full.ap(), "dbg", r=[D_y_full])
                P.end_phase()

        def phase_B_ssm(layer):
            j = layer // 2
            with ExitStack() as st:
                HN = P.sb(st, "sHN", [128, 4, 4, 512], BF16)
                wsl = Rot([P.sb(st, f"sw{i}", [128, 16, 256], BF16) for i in range(3)])
                CW = P.sb(st, "CW", [128, 24, 4], F32)
                CB_ = P.sb(st, "CBias", [128, 24], F32)
                DTB = P.sb(st, "DTB", [128, 32], F32)
                AB = P.sb(st, "AB", [128, 32], F32)
                DB = P.sb(st, "DB", [128, 32], F32)
                NG = P.sb(st, "NG", [128, 2048], F32)
                HS = P.sb(st, "HS", [128, 24, 4], F32)
                Ee = Rot([P.sb(st, f"Ee{i}", [128, 516], F32) for i in range(2)])
                acc = Rot([P.sb(st, f"acc{i}", [128, 512], F32) for i in range(2)])
                XBC = P.sb(st, "XBC", [128, 24, 512], BF16)
                ZS = P.sb(st, "ZS", [128, 4, 2048], BF16)
                DTr = P.sb(st, "DTr", [128, 4, 32], F32)
                d8 = P.sb(st, "d8", [128, 8, 32], F32)
                Xg = Rot([P.sb(st, f"Xg{i}", [128, 8, 128], F32) for i in range(2)])
                xtok = P.sb(st, "xtok", [128, 2048], BF16)
                xdt = P.sb(st, "xdt", [128, 2048], BF16)
                xdtd = P.sb(st, "xdtd", [128, 2048], BF16)
                Btok = P.sb(st, "Btok", [128, 4, 128], BF16)
                CBm4 = [P.sb(st, f"CBm{i}", [128, 128], F32) for i in range(4)]
                Eg = Rot([P.sb(st, f"Eg{i}", [128, 4, 128], F32) for i in range(4)])
                Mt4 = [P.sb(st, f"Mt{i}", [128, 8, 128], BF16) for i in range(4)]
                deferred = [None]
                Y = P.sb(st, "Y", [128, 512], F32)
                t1 = P.sb(st, "t1", [128, 512], F32)
                t3 = P.sb(st, "t3", [128, 512], F32)
                ysq = P.sb(st, "ysq", [128, 512], F32)
                ynb4 = [P.sb(st, f"ynb{i}", [128, 512], BF16) for i in range(4)]
                g4 = P.sb(st, "g4", [128, 8], F32)
                stf = P.sb(st, "stf", [128, 2048], F32)
                stb = P.sb(st, "stb", [128, 2048], BF16)
                yT = Rot([P.sb(st, f"yT{i}", [128, 16, 512], BF16) for i in range(2)])
                psr = Rot([P.ps(st, f"spsr{i}", [128, 512], F32) for i in range(6)])
                psb = Rot([P.ps(st, f"spsb{i}", [128, 8, 128], BF16) for i in range(2)])

                bc = lambda d: d.ap()[j, :].partition_broadcast(128)
                P.dma("sp", CW[:, :, :], convw_d.ap()[j].rearrange("p (c k) -> p c k", k=4), "const", w=[CW])
                P.dma("sp", CB_[:, :], convb_d.ap()[j], "const", w=[CB_])
                P.dma("sp", DTB[:, :], bc(dtb_d), "const", w=[DTB])
                P.dma("sp", AB[:, :], bc(alog_d), "const", w=[AB])
                P.dma("sp", DB[:, :], bc(dsk_d), "const", w=[DB])
                P.dma("sp", NG[:, :], bc(ngain_d), "const", w=[NG])
                P.op("act", lambda e: e.activation(out=AB[:, :], in_=AB[:, :], func=AF.Exp), r=[AB], w=[AB])
                P.op("act", lambda e: e.mul(out=AB[:, :], in_=AB[:, :], mul=-1.0), r=[AB], w=[AB])
                P.op("pool", lambda e: e.memset(HS[:, :, :], 0.0), w=[HS])
                P.op("pool", lambda e: e.memset(stf[:, :], 0.0), w=[stf])
                P.op("pool", lambda e: e.memset(stb[:, :], 0.0), w=[stb])
                Wl = win.ap()[j]

                try:
                  chk(1)
                  for g in range(8):
                      load_hn(HN, g)
                      YT = yT.next()
                      hnr = lambda dc: HN[:, dc // 4, dc % 4, :]
                      for cp in range(12):
                          W = wsl.next()
                          load_w(W, Wl, 0, cp * 256, 256)
                          for cl in range(2):
                              ch = cp * 2 + cl
                              pp = psr.next()

                              def mmx(e, pp=pp, W=W, cl=cl):
                                  for dc in range(16):
                                      last = e.matmul(pp[:, :], lhsT=W[:, dc, cl * 128:(cl + 1) * 128], rhs=hnr(dc),
                                                      start=(dc == 0), stop=(dc == 15))
                                  return last
                              P.op("pe", mmx, r=[HN, W], w=[pp])
                              chk(2)
                              E = Ee.next()
                              A_ = acc.next()
                              P.op("act", lambda e, E=E, ch=ch: e.copy(out=E[:, 0:3], in_=HS[:, ch, 0:3]), r=[HS], w=[E])
                              P.op("act", lambda e, E=E, pp=pp: e.copy(out=E[:, 3:515], in_=pp[:, :]), r=[pp], w=[E])
                              P.op("act", lambda e, E=E, ch=ch: e.copy(out=HS[:, ch, 0:3], in_=E[:, 512:515]), r=[E], w=[HS])
                              P.op("dve", lambda e, E=E, A_=A_, ch=ch: e.tensor_scalar(
                                  out=A_[:, :], in0=E[:, 0:512], scalar1=CW[:, ch, 0:1], scalar2=None, op0=ALU.mult),
                                  r=[E, CW], w=[A_])
                              for k in range(1, 4):
                                  P.op("dve", lambda e, E=E, A_=A_, ch=ch, k=k: e.scalar_tensor_tensor(
                                      out=A_[:, :], in0=E[:, k:k + 512], scalar=CW[:, ch, k:k + 1], in1=A_[:, :],
                                      op0=ALU.mult, op1=ALU.add), r=[E, CW, A_], w=[A_])
                              P.op("act", lambda e, A_=A_, ch=ch: e.activation(out=XBC[:, ch, :], in_=A_[:, :], func=AF.Silu,
                                                                              bias=CB_[:, ch:ch + 1], scale=1.0), r=[A_, CB_], w=[XBC])
                      chk(3)
                      for zc in range(8):
                          W = wsl.next()
                          load_w(W, Wl, 0, 3072 + zc * 256, 256)
                          for t4 in range(4):
                              pp = psr.next()

                              def mmz(e, pp=pp, W=W, t4=t4):
                                  for dc in range(16):
                                      last = e.matmul(pp[:, 0:256], lhsT=HN[:, dc // 4, dc % 4, t4 * 128:(t4 + 1) * 128],
                                                      rhs=W[:, dc, :], start=(dc == 0), stop=(dc == 15))
                                  return last
                              P.op("pe", mmz, r=[HN, W], w=[pp])
                              P.op("act", lambda e, pp=pp, t4=t4, zc=zc: e.activation(
                                  out=ZS[:, t4, zc * 256:(zc + 1) * 256], in_=pp[:, 0:256], func=AF.Silu), r=[pp], w=[ZS])
                      wdt = wsl.next()
                      load_w(wdt, Wl, 0, 5120, 256)
                      for t4 in range(4):
                          pp = psr.next()

                          def mmd(e, pp=pp, t4=t4, wdt=wdt):
                              for dc in range(16):
                                  last = e.matmul(pp[:, 0:32], lhsT=HN[:, dc // 4, dc % 4, t4 * 128:(t4 + 1) * 128],
                                                  rhs=wdt[:, dc, 0:32], start=(dc == 0), stop=(dc == 15))
                              return last
                          P.op("pe", mmd, r=[HN, wdt], w=[pp])
                          P.op("dve", lambda e, pp=pp, t4=t4: e.tensor_tensor(out=DTr[:, t4, :], in0=pp[:, 0:32], in1=DTB[:, :], op=ALU.add),
                               r=[pp, DTB], w=[DTr])
                      chk(4)
                      for t4 in range(4):
                          ts_ = slice(t4 * 128, (t4 + 1) * 128)
                          xx = DTr[:, t4, :]
                          P.op("act", lambda e, xx=xx: e.activation(out=d8[:, 0, :], in_=xx, func=AF.Abs), r=[DTr], w=[d8])
                          P.op("act", lambda e: e.activation(out=d8[:, 1, :], in_=d8[:, 0, :], func=AF.Exp, scale=-1.0), r=[d8], w=[d8])
                          P.op("act", lambda e: e.activation(out=d8[:, 2, :], in_=d8[:, 1, :], func=AF.Ln, bias=EPS_T[:, 1:2], scale=1.0),
                               r=[d8, EPS_T], w=[d8])
                          P.op("dve", lambda e, xx=xx: e.scalar_tensor_tensor(out=d8[:, 3, :], in0=xx, scalar=0.0, in1=d8[:, 2, :],
                                                                              op0=ALU.max, op1=ALU.add), r=[DTr, d8], w=[d8])
                          P.op("dve", lambda e: e.tensor_tensor(out=d8[:, 4, :], in0=d8[:, 3, :], in1=AB[:, :], op=ALU.mult),
                               r=[d8, AB], w=[d8])
                          pa = psr.next()
                          P.op("pe", lambda e, pa=pa: e.matmul(pa[:, 0:32], lhsT=tri_f[:, :], rhs=d8[:, 4, :], start=True, stop=True),
                               r=[d8, tri_f], w=[pa])
                          P.op("pe", lambda e, pa=pa: e.matmul(pa[:, 32:64], lhsT=ones_f[:, :], rhs=d8[:, 4, :], start=True, stop=True),
                               r=[d8, ones_f], w=[pa])
                          P.op("act", lambda e, pa=pa: e.copy(out=d8[:, 5, :], in_=pa[:, 0:32]), r=[pa], w=[d8])
                          P.op("act", lambda e: e.activation(out=d8[:, 6, :], in_=d8[:, 5, :], func=AF.Exp), r=[d8], w=[d8])
                          P.op("act", lambda e, pa=pa: e.activation(out=d8[:, 7, :], in_=pa[:, 32:64], func=AF.Exp), r=[pa], w=[d8])
                          P.op("dve", lambda e, pa=pa: e.tensor_tensor(out=d8[:, 0, :], in0=pa[:, 32:64], in1=d8[:, 5, :], op=ALU.subtract),
                               r=[pa, d8], w=[d8])
                          P.op("act", lambda e: e.activation(out=d8[:, 1, :], in_=d8[:, 0, :], func=AF.Exp), r=[d8], w=[d8])
                          chk(5)
                          for half in range(2):
                              pb = psb.next()

                              def trx(e, pb=pb, half=half, ts_=ts_):
                                  for i in range(8):
                                      last = e.transpose(out=pb[:, i, :], in_=XBC[:, half * 8 + i, ts_], identity=ident_bf[:, :])
                                  return last
                              P.op("pe", trx, r=[XBC, ident_bf], w=[pb])
                              hs = slice(half * 1024, (half + 1) * 1024)
                              P.op("act", lambda e, pb=pb, hs=hs: e.copy(out=xtok[:, hs], in_=pb[:, :, :].rearrange("p a b -> p (a b)")),
                                   r=[pb], w=[xtok])
                              P.op("dve", lambda e, pb=pb, hs=hs, half=half: e.tensor_tensor(
                                  out=xdt[:, hs].rearrange("p (h d) -> p h d", d=64),
                                  in0=pb[:, :, :].rearrange("p a (b d) -> p (a b) d", d=64),
                                  in1=d8[:, 3, half * 16:(half + 1) * 16].unsqueeze(2).broadcast_to([128, 16, 64]), op=ALU.mult),
                                  r=[pb, d8], w=[xdt])
                          P.op("dve", lambda e: e.tensor_tensor(
                              out=xdtd[:, :].rearrange("p (h d) -> p h d", d=64), in0=xdt[:, :].rearrange("p (h d) -> p h d", d=64),
                              in1=d8[:, 1, :].unsqueeze(2).broadcast_to([128, 32, 64]), op=ALU.mult), r=[xdt, d8], w=[xdtd])
                          pb = psb.next()

                          def trb(e, pb=pb, ts_=ts_):
                              for i in range(4):
                                  last = e.transpose(out=pb[:, i, :], in_=XBC[:, 16 + i, ts_], identity=ident_bf[:, :])
                              return last
                          P.op("pe", trb, r=[XBC, ident_bf], w=[pb])
                          P.op("act", lambda e, pb=pb: e.copy(out=Btok[:, :, :], in_=pb[:, 0:4, :]), r=[pb], w=[Btok])
                          chk(6)
                          v3 = lambda ap: ap.rearrange("p (h d) -> p h d", d=64)
                          pend = None

                          def emit_mt(pd):
                              gg_, egs = pd
                              for q, EG in enumerate(egs):
                                  P.op("dve", lambda e, EG=EG, q=q, gg_=gg_: e.tensor_tensor(
                                      out=Mt4[gg_][:, q * 4:(q + 1) * 4, :], in0=EG[:, :, :],
                                      in1=CBm4[gg_][:, :].unsqueeze(1).broadcast_to([128, 4, 128]), op=ALU.mult),
                                      r=[EG, CBm4[gg_]], w=[Mt4[gg_]])
                          for gg in range(4):
                              pcb = psr.next()
                              P.op("pe", lambda e, pcb=pcb, gg=gg, ts_=ts_: e.matmul(
                                  pcb[:, 0:128], lhsT=XBC[:, 16 + gg, ts_], rhs=XBC[:, 20 + gg, ts_], start=True, stop=True),
                                  r=[XBC], w=[pcb])
                              P.op("dve", lambda e, pcb=pcb, gg=gg: e.tensor_tensor(out=CBm4[gg][:, :], in0=pcb[:, 0:128], in1=tri_f[:, :], op=ALU.mult),
                                   r=[pcb, tri_f], w=[CBm4[gg]])
                              X = Xg.next()
                              P.op("dve", lambda e, X=X, gg=gg: e.tensor_tensor(
                                  out=X[:, :, :], in0=tri_f[:, :].unsqueeze(1).broadcast_to([128, 8, 128]),
                                  in1=d8[:, 4, gg * 8:(gg + 1) * 8].unsqueeze(2).broadcast_to([128, 8, 128]), op=ALU.mult),
                                  r=[tri_f, d8], w=[X])
                              egs = []
                              for q in range(2):
                                  psg = psr.next()
                                  P.op("pe", lambda e, psg=psg, X=X, q=q: e.matmul(
                                      psg[:, :], lhsT=U_f[:, :], rhs=X[:, q * 4:(q + 1) * 4, :].rearrange("p a b -> p (a b)"),
                                      start=True, stop=True), r=[X, U_f], w=[psg])
                                  EG = Eg.next()
                                  P.op("act", lambda e, psg=psg, EG=EG: e.activation(
                                      out=EG[:, :, :].rearrange("p a b -> p (a b)"), in_=psg[:, :], func=AF.Exp), r=[psg], w=[EG])
                                  egs.append(EG)
                              if pend is not None:
                                  emit_mt(pend)
                              pend = (gg, egs)
                          emit_mt(pend)
                          if deferred[0] is not None:
                              deferred[0]()
                              deferred[0] = None
                          chk(7)
                          for gg in range(4):
                              gs_ = slice(gg * 512, (gg + 1) * 512)
                              pyd = psr.next()

                              def mmy(e, pyd=pyd, gg=gg):
                                  for hh in range(8):
                                      h = gg * 8 + hh
                                      last = e.matmul(pyd[:, hh * 64:(hh + 1) * 64], lhsT=Mt4[gg][:, hh, :], rhs=xdt[:, h * 64:(h + 1) * 64],
                                                      start=True, stop=True)
                                  return last
                              P.op("pe", mmy, r=[Mt4[gg], xdt], w=[pyd])
                              pcs = psr.next()
                              P.op("pe", lambda e, pcs=pcs, gg=gg, ts_=ts_, gs_=gs_: e.matmul(
                                  pcs[:, :], lhsT=XBC[:, 20 + gg, ts_], rhs=stb[:, gs_], start=True, stop=True), r=[XBC, stb], w=[pcs])
                              pds = psr.next()
                              P.op("pe", lambda e, pds=pds, gg=gg, gs_=gs_: e.matmul(
                                  pds[:, :], lhsT=Btok[:, gg, :], rhs=xdtd[:, gs_], start=True, stop=True), r=[Btok, xdtd], w=[pds])
                              b8 = lambda row, gg=gg: d8[:, row, gg * 8:(gg + 1) * 8].unsqueeze(2).broadcast_to([128, 8, 64])
                              P.op("dve", lambda e, pcs=pcs, b8=b8: e.tensor_tensor(out=v3(t1[:, :]), in0=v3(pcs[:, :]), in1=b8(6), op=ALU.mult),
                                   r=[pcs, d8], w=[t1])
                              P.op("dve", lambda e, pyd=pyd: e.tensor_tensor(out=Y[:, :], in0=pyd[:, :], in1=t1[:, :], op=ALU.add),
                                   r=[pyd, t1], w=[Y])
                              P.op("dve", lambda e, gs_=gs_, b8=b8: e.tensor_tensor(out=v3(t1[:, :]), in0=v3(stf[:, gs_]), in1=b8(7), op=ALU.mult),
                                   r=[stf, d8, Y], w=[t1])
                              P.op("dve", lambda e, pds=pds, gs_=gs_: e.tensor_tensor(out=stf[:, gs_], in0=pds[:, :], in1=t1[:, :], op=ALU.add),
                                   r=[pds, t1], w=[stf])
                              P.op("act", lambda e, gs_=gs_: e.copy(out=stb[:, gs_], in_=stf[:, gs_]), r=[stf], w=[stb])
                              chk(8)
                              P.op("dve", lambda e, gs_=gs_, gg=gg: e.tensor_tensor(
                                  out=v3(t3[:, :]), in0=v3(xtok[:, gs_]),
                                  in1=DB[:, gg * 8:(gg + 1) * 8].unsqueeze(2).broadcast_to([128, 8, 64]), op=ALU.mult), r=[xtok, DB], w=[t3])
                              P.op("dve", lambda e: e.tensor_tensor(out=Y[:, :], in0=Y[:, :], in1=t3[:, :], op=ALU.add), r=[Y, t3], w=[Y])
                              P.op("dve", lambda e, t4=t4, gs_=gs_: e.tensor_tensor(out=Y[:, :], in0=Y[:, :], in1=ZS[:, t4, gs_], op=ALU.mult),
                                   r=[Y, ZS], w=[Y])
                              P.op("act", lambda e: e.activation(out=ysq[:, :], in_=Y[:, :], func=AF.Square, accum_out=g4[:, 0:1]),
                                   r=[Y], w=[ysq, g4])
                              P.op("dve", lambda e: e.tensor_scalar(out=g4[:, 1:2], in0=g4[:, 0:1], scalar1=1.0 / 512, scalar2=EPS,
                                                                    op0=ALU.mult, op1=ALU.add), r=[g4], w=[g4])
                              P.op("act", lambda e: e.activation(out=g4[:, 2:3], in_=g4[:, 1:2], func=AF.Sqrt), r=[g4], w=[g4])
                              P.op("dve", lambda e: e.reciprocal(out=g4[:, 3:4], in_=g4[:, 2:3]), r=[g4], w=[g4])
                              P.op("dve", lambda e, gs_=gs_, gg=gg: e.scalar_tensor_tensor(
                                  out=ynb4[gg][:, :], in0=Y[:, :], scalar=g4[:, 3:4], in1=NG[:, gs_], op0=ALU.mult, op1=ALU.mult),
                                  r=[Y, g4, NG], w=[ynb4[gg]])

                          def sweep3(YT=YT, ts_=ts_):
                              for gg in range(4):
                                  pb = psb.next()

                                  def try_(e, pb=pb, gg=gg):
                                      for i in range(4):
                                          last = e.transpose(out=pb[:, i, :], in_=ynb4[gg][:, i * 128:(i + 1) * 128], identity=ident_bf[:, :])
                                      return last
                                  P.op("pe", try_, r=[ynb4[gg], ident_bf], w=[pb])
                                  P.op("act", lambda e, pb=pb, YT=YT, gg=gg, ts_=ts_: e.copy(out=YT[:, gg * 4:(gg + 1) * 4, ts_], in_=pb[:, 0:4, :]),
                                       r=[pb], w=[YT])
                          deferred[0] = sweep3
                      if deferred[0] is not None:
                          deferred[0]()
                          deferred[0] = None
                      chk(9)
                      yv = y_own.ap().rearrange("a b t -> (a b) t")[:, g * 512:(g + 1) * 512].rearrange("(c p) t -> p c t", p=128)
                      P.dma("sp", yv, YT[:, :, :], "d_" + YT.name, r=[YT], w=[D_y_own])
                except _Stop:
                    pass
                if mode == "full":
                    allgather(y_own, y_full, 8, D_y_own, D_y_full)
                if debug and layer == n_layers - 1:
                    P.dma("sp", dbg_y.ap(), y_full.ap(), "dbg", r=[D_y_full])
                P.end_phase()

        def phase_C(layer, tt2):
            j = layer // 2
            attn = (layer % 2 == 0)
            KC = 16 if attn else 32
            ncc = 4 if attn else 8
            Wo = wo.ap()[j] if attn else wout.ap()[j]
            src = xT if layer == 0 else res
            last = (layer == n_layers - 1)
            dst = outT if last else res
            t0 = tt2 * 1024
            with ExitStack() as so:
                hn2 = P.sb(so, "hn2", [128, 16, 1024], BF16)
                with ExitStack() as st:
                    R = P.sb(st, "C_R", [128, 16, 1024], F32)
                    yT = P.sb(st, "C_yT", [128, KC, 1024], BF16)
                    lo = Rot([P.sb(st, f"C_lo{i}", [128, 2, 1024], BF16) for i in range(2)])
                    hi = Rot([P.sb(st, f"C_hi{i}", [128, 2, 1024], BF16) for i in range(2)])
                    wsl = Rot([P.sb(st, f"C_w{i}", [128, 16, 256], BF16) for i in range(2)])
                    sq = Rot([P.sb(st, f"C_sq{i}", [128, 1024], F32) for i in range(1)])
                    rs = P.sb(st, "C_rs", [128, 1024], F32)
                    ri = rs
                    pp = [[P.ps(st, f"C_p{a}{b}", [128, 512], F32) for b in range(2)] for a in range(2)]
                    pn = [P.ps(st, f"C_pn{b}", [128, 512], F32) for b in range(2)]
                    P.dma("sp", R[:, :, :], src.ap()[:, :, t0:t0 + 1024].rearrange("c p t -> p c t"), "d_C_R", r=[D_res[tt2]], w=[R])
                    for r in range(2):
                        for c in range(ncc):
                            L, H = lo.next(), hi.next()
                            rows = slice(r * 256, (r + 1) * 256)
                            P.dma("sp", L[:, :, :], y_full.ap()[c, rows, t0:t0 + 1024].rearrange("(h p) t -> p h t", p=128),
                                  "d_" + L.name, r=[D_y_full], w=[L])
                            P.dma("sp", H[:, :, :], y_full.ap()[c, rows, 2048 + t0:2048 + t0 + 1024].rearrange("(h p) t -> p h t", p=128),
                                  "d_" + H.name, r=[D_y_full], w=[H])
                            kc0 = r * (KC // 2) + c * 2
                            P.op("dve", lambda e, H=H: e.tensor_scalar(out=H[:, :, :], in0=H[:, :, :], scalar1=selv[:, 0:1], scalar2=None,
                                                                      op0=ALU.mult), r=[H, selv], w=[H])
                            P.op("dve", lambda e, L=L, H=H, kc0=kc0: e.scalar_tensor_tensor(
                                out=yT[:, kc0:kc0 + 2, :], in0=L[:, :, :], scalar=selv[:, 1:2], in1=H[:, :, :], op0=ALU.mult, op1=ALU.add),
                                r=[L, selv, H], w=[yT])
                    NKQ = KC // 16
                    for dcp in range(8):
                        for kq in range(NKQ):
                            W = wsl.next()
                            load_w(W, Wo, kq * 16, dcp * 256, 256)

                            def mmo(e, kq=kq, W=W):
                                for dcl in range(2):
                                    for sub in range(2):
                                        for c16 in range(16):
                                            last_ = e.matmul(pp[dcl][sub][:, :], lhsT=W[:, c16, dcl * 128:(dcl + 1) * 128],
                                                             rhs=yT[:, kq * 16 + c16, sub * 512:(sub + 1) * 512],
                                                             start=(kq == 0 and c16 == 0), stop=(kq == NKQ - 1 and c16 == 15))
                                return last_
                            P.op("pe", mmo, r=[yT, W], w=[pp[0][0], pp[0][1], pp[1][0], pp[1][1]])
                        for dcl in range(2):
                            for sub in range(2):
                                dc = dcp * 2 + dcl
                                P.op("dve", lambda e, dc=dc, sub=sub, dcl=dcl: e.tensor_tensor(
                                    out=R[:, dc, sub * 512:(sub + 1) * 512], in0=R[:, dc, sub * 512:(sub + 1) * 512],
                                    in1=pp[dcl][sub][:, :], op=ALU.add), r=[R, pp[dcl][sub]], w=[R])
                    P.dma("sp", res.ap()[:, :, t0:t0 + 1024].rearrange("c p t -> p c t"), R[:, :, :], "d_C_R", r=[R], w=[D_res[tt2]])
                    if debug and last:
                        P.dma("sp", dbg_mix.ap()[:, :, t0:t0 + 1024].rearrange("c p t -> p c t"), R[:, :, :], "d_C_R", r=[R])
                    for c in range(16):
                        S_ = sq.next()
                        P.op("act", lambda e, S_=S_, c=c: e.activation(out=S_[:, :], in_=R[:, c, :], func=AF.Square), r=[R], w=[S_])
                        for sub in range(2):
                            P.op("pe", lambda e, S_=S_, c=c, sub=sub: e.matmul(
                                pn[sub][:, :], lhsT=ones_f[:, :], rhs=S_[:, sub * 512:(sub + 1) * 512], start=(c == 0), stop=(c == 15)),
                                r=[S_, ones_f], w=[pn[sub]])
                    for sub in range(2):
                        P.op("act", lambda e, sub=sub: e.activation(out=rs[:, sub * 512:(sub + 1) * 512], in_=pn[sub][:, :], func=AF.Sqrt,
                                                                     bias=EPS_AP[:, 0:1], scale=1.0 / 2048), r=[pn[sub], EPS_T], w=[rs])
                    P.op("dve", lambda e: e.reciprocal(out=rs[:, :], in_=rs[:, :]), r=[rs], w=[rs])

                    def nrm(e):
                        for c in range(16):
                            last_ = e.scalar_tensor_tensor(out=hn2[:, c, :], in0=R[:, c, :], scalar=mlpg[:, layer * 16 + c:layer * 16 + c + 1],
                                                           in1=ri[:, :], op0=ALU.mult, op1=ALU.mult)
                        return last_
                    P.op("dve", nrm, r=[R, ri, mlpg], w=[hn2])
                    P.end_phase()
                with ExitStack() as st:
                    h1 = P.sb(st, "h1", [128, 64, 1024], BF16)
                    wsl = Rot([P.sb(st, f"M_w{i}", [128, 16, 256], BF16) for i in range(3)])
                    rl = Rot([P.sb(st, f"M_rl{i}", [128, 512], F32) for i in range(2)])
                    rc = Rot([P.sb(st, f"M_rc{i}", [128, 512], F32) for i in range(3)])
                    psr = Rot([P.ps(st, f"M_p{i}", [128, 512], F32) for i in range(8)])
                    W1 = w1.ap()[layer]
                    W2 = w2.ap()[layer]
                    for fp in range(32):
                        W = wsl.next()
                        load_w(W, W1, 0, fp * 256, 256)
                        for fl in range(2):
                            for sub in range(2):
                                p_ = psr.next()

                                def mm1(e, p_=p_, W=W, fl=fl, sub=sub):
                                    for dc in range(16):
                                        last_ = e.matmul(p_[:, :], lhsT=W[:, dc, fl * 128:(fl + 1) * 128],
                                                         rhs=hn2[:, dc, sub * 512:(sub + 1) * 512], start=(dc == 0), stop=(dc == 15))
                                    return last_
                                P.op("pe", mm1, r=[hn2, W], w=[p_])
                                RL = rl.next()
                                P.op("act", lambda e, p_=p_, RL=RL: e.activation(out=RL[:, :], in_=p_[:, :], func=AF.Relu), r=[p_], w=[RL])
                                P.op("dve", lambda e, RL=RL, fp=fp, fl=fl, sub=sub: e.tensor_tensor(
                                    out=h1[:, fp * 2 + fl, sub * 512:(sub + 1) * 512], in0=RL[:, :], in1=RL[:, :], op=ALU.mult), r=[RL], w=[h1])
                    for dcp in range(8):
                        pb4 = [[psr.next() for _ in range(2)] for _ in range(2)]
                        for kq in range(4):
                            W = wsl.next()
                            load_w(W, W2, kq * 16, dcp * 256, 256)

                            def mm2(e, kq=kq, W=W, pb4=pb4):
                                for dcl in range(2):
                                    for sub in range(2):
                                        for c16 in range(16):
                                            last_ = e.matmul(pb4[dcl][sub][:, :], lhsT=W[:, c16, dcl * 128:(dcl + 1) * 128],
                                                             rhs=h1[:, kq * 16 + c16, sub * 512:(sub + 1) * 512],
                                                             start=(kq == 0 and c16 == 0), stop=(kq == 3 and c16 == 15))
                                return last_
                            P.op("pe", mm2, r=[h1, W], w=[pb4[0][0], pb4[0][1], pb4[1][0], pb4[1][1]])
                        for dcl in range(2):
                            for sub in range(2):
                                p_ = pb4[dcl][sub]
                                dc = dcp * 2 + dcl
                                RC = rc.next()
                                tsl = slice(t0 + sub * 512, t0 + (sub + 1) * 512)
                                P.dma("sp", RC[:, :], res.ap()[dc, :, tsl], "d_" + RC.name, r=[D_res[tt2]], w=[RC])
                                P.op("dve", lambda e, RC=RC, p_=p_: e.tensor_tensor(out=RC[:, :], in0=RC[:, :], in1=p_[:, :], op=ALU.add),
                                     r=[RC, p_], w=[RC])
                                P.dma("sp", dst.ap()[dc, :, tsl], RC[:, :], "d_" + RC.name, r=[RC], w=[D_out if last else D_res[tt2]])
                    P.end_phase()

        EPS_T = P.sb(gs, "eps_t", [128, 2], F32)
        EPS_AP = EPS_T
        P.op("pool", lambda e: e.memset(EPS_T[:, 0:1], EPS), w=[EPS_T])
        P.op("pool", lambda e: e.memset(EPS_T[:, 1:2], 1.0), w=[EPS_T])

        if mode == "ssm":
            phase_B_ssm(1)
            return nc
        for layer in range(n_layers):
            phase_A(layer)
            if layer % 2 == 0:
                phase_B_attn(layer)
            else:
                phase_B_ssm(layer)
            for tt2 in range(2):
                phase_C(layer, tt2)
    return nc


def _prep_inputs(inp):
    f = lambda a: np.ascontiguousarray(np.asarray(a, dtype=np.float32))
    x = f(inp["x"])
    pos = np.arange(4096, dtype=np.float32)[:, None]
    inv = (np.float32(500000.0) ** (-np.arange(0, 32, 2, dtype=np.float32) / np.float32(32))).astype(np.float32)
    ang = (pos * inv[None, :]).astype(np.float32)
    tab = lambda a: f(a.reshape(32, 128, 16).transpose(1, 0, 2).reshape(128, 512))
    cos, sin = tab(np.cos(ang)), tab(np.sin(ang))
    gl = lambda a: f(a.reshape(4, 16, 128).transpose(2, 0, 1).reshape(128, 64))
    mixg, mlpg = gl(f(inp["mixer_norm"])), gl(f(inp["mlp_norm"]))
    wqkv_full, win_full = f(inp["attn_w_qkv"]), f(inp["ssm_w_in"])
    convw_full, convb_full = f(inp["ssm_conv_w"]), f(inp["ssm_conv_b"])
    shared = {
        "mixg": mixg, "mlpg": mlpg, "qg": f(inp["attn_q_norm"]), "kg": f(inp["attn_k_norm"]),
        "lam": f(inp["attn_lambda"]).reshape(2, 512), "subln": f(inp["attn_subln"]), "wo": f(inp["attn_w_o"]),
        "wout": f(inp["ssm_w_out"]), "w1": f(inp["mlp_w1"]), "w2": f(inp["mlp_w2"]), "cos": cos, "sin": sin,
    }
    per_rank = []
    for h in range(2):
        cols = []
        for hd in range(4 * h, 4 * h + 4):
            cols += [np.arange(hd * 256, (hd + 1) * 256), 2048 + np.arange(hd * 256, (hd + 1) * 256),
                     4096 + np.arange(hd * 256, (hd + 1) * 256)]
        cols = np.concatenate(cols)
        xs = 4096 + np.arange(h * 2048, (h + 1) * 2048)
        Bs = 8192 + np.arange(h * 512, (h + 1) * 512)
        Cs = 8192 + 1024 + np.arange(h * 512, (h + 1) * 512)
        zs = np.arange(h * 2048, (h + 1) * 2048)
        dts = 10240 + np.arange(h * 32, (h + 1) * 32)
        icol = np.concatenate([xs, Bs, Cs, zs, dts, np.zeros(224, np.int64)])
        cch = np.concatenate([xs, Bs, Cs]) - 4096
        cw = convw_full[:, :, cch]
        cw = f(cw.transpose(0, 2, 1).reshape(2, 24, 128, 4).transpose(0, 2, 1, 3).reshape(2, 128, 96))
        cb = f(convb_full[:, cch].reshape(2, 24, 128).transpose(0, 2, 1))
        hs = slice(h * 32, (h + 1) * 32)
        sel = np.zeros((128, 2), np.float32)
        sel[:, 0] = h
        sel[:, 1] = 1 - h
        per_rank.append({
            "wqkv": f(wqkv_full[:, :, cols]), "win": f(win_full[:, :, icol]), "convw": cw, "convb": cb,
            "dtb": f(inp["ssm_dt_bias"])[:, hs].copy(), "alog": f(inp["ssm_a_log"])[:, hs].copy(), "dsk": f(inp["ssm_d"])[:, hs].copy(),
            "ngain": f(inp["ssm_norm"])[:, h * 2048:(h + 1) * 2048].copy(), "selv": sel,
        })
    maps = []
    for c in range(8):
        b, h = c // 2, c % 2
        m = dict(shared)
        m.update(per_rank[h])
        m["xT"] = f(x[b, h * 2048:(h + 1) * 2048, :].T.reshape(16, 128, 2048))
        maps.append(m)
    return maps


_NC_CACHE = {}


def kernel(**inputs):
    maps = _prep_inputs(inputs)
    if "nc" not in _NC_CACHE:
        _NC_CACHE["nc"] = build(4)
    res = run_bass_kernel_spmd(_NC_CACHE["nc"], maps, core_ids=list(range(8)))
    out = np.empty((4, 4096, 2048), np.float32)
    for c in range(8):
        b, h = c // 2, c % 2
        o = np.asarray(res.results[c]["outT"]).reshape(2048, 2048)
        out[b, h * 2048:(h + 1) * 2048, :] = o.T
    return out
```

```python
import math
from contextlib import ExitStack
import numpy as np
import concourse.bass as bass
import concourse.mybir as mybir
from concourse.bass_utils import run_bass_kernel_spmd

F32 = mybir.dt.float32
BF16 = mybir.dt.bfloat16
ALU = mybir.AluOpType
AF = mybir.ActivationFunctionType
AX = mybir.AxisListType

EPS = 1e-6
GROUPS = [[0, 1], [2, 3], [4, 5], [6, 7]]
CE = ("pe", "act", "dve", "pool")


def lambda_init(i):
    return 0.8 - 0.6 * math.exp(-0.3 * i)


SSM_STOP = [0]


class _Stop(Exception):
    pass


def chk(n):
    if SSM_STOP[0] == n:
        raise _Stop()


class Buf:
    __slots__ = ("w", "r")

    def __init__(self):
        self.w = None
        self.r = {}


class T:
    def __init__(self, t, name, psum=False):
        self.t = t
        self.name = name
        self.b = Buf()
        self.psum = psum

    def __getitem__(self, k):
        return self.t[k]


class Prog:
    ENG = ("pe", "act", "dve", "pool", "sp")

    def __init__(self, nc, stack):
        self.nc = nc
        self.stack = stack
        self.sem = {}
        self.cnt = {}
        self.q = {e: [] for e in self.ENG}
        self.known = {e: {} for e in self.ENG}
        for e in CE:
            self._sem(e)

    def _sem(self, key):
        if key not in self.sem:
            self.sem[key] = self.stack.enter_context(self.nc.semaphore("s_" + key))
            self.cnt[key] = 0
        return self.sem[key]

    def _need(self, eng, key, val, waits):
        if key not in CE:
            val = self.cnt[key]
        if key == "pe" and eng == "pe":
            return
        if self.known[eng].get(key, 0) < val:
            self.known[eng][key] = val
            waits[key] = val

    def op(self, eng, fn, r=(), w=(), dma=None, inc=16):
        w = list(w) + [t for t in r if t.psum and t not in w]
        r = [t for t in r if not t.psum]
        waits = {}
        for t in r:
            if t.b.w is not None:
                self._need(eng, t.b.w[0], t.b.w[1], waits)
        for t in w:
            if t.b.w is not None:
                self._need(eng, t.b.w[0], t.b.w[1], waits)
            for k, v in t.b.r.items():
                self._need(eng, k, v, waits)
        if dma is None:
            self.cnt[eng] += 1
            ev = (eng, self.cnt[eng])
        else:
            self._sem(dma)
            self.cnt[dma] += inc
            ev = (dma, self.cnt[dma])
        for t in w:
            t.b.w = ev
            t.b.r = {}
        for t in r:
            if t not in w:
                t.b.r[ev[0]] = ev[1]
        self.q[eng].append((list(waits.items()), fn, ev, dma))

    def dma(self, q, out_ap, in_ap, key, r=(), w=()):
        def fn(e, sem):
            e.dma_start(out=out_ap, in_=in_ap).then_inc(sem, 16)
        self.op(q, fn, r=r, w=w, dma=key, inc=16)

    def barrier(self):
        for eng in self.ENG:
            waits = []
            for key, c in self.cnt.items():
                if c > 0 and self.known[eng].get(key, 0) < c and not (key == "pe" and eng == "pe"):
                    self.known[eng][key] = c
                    waits.append((key, c))
            self.q[eng].append((waits, None, None, None))

    def flush(self):
        nc = self.nc
        q = self.q
        self.q = {e: [] for e in self.ENG}
        sem = self.sem

        def mk(name):
            def body(e):
                for waits, fn, ev, dma in q[name]:
                    for key, val in waits:
                        e.wait_ge(sem[key], val)
                    if fn is None:
                        continue
                    if dma is None:
                        fn(e).then_inc(sem[ev[0]], 1)
                    else:
                        fn(e, sem[dma])
            return body

        with nc.Block() as block:
            block.tensor(mk("pe"))
            block.scalar(mk("act"))
            block.vector(mk("dve"))
            block.gpsimd(mk("pool"))
            block.sync(mk("sp"))

    def end_phase(self):
        self.barrier()
        self.flush()

    def sb(self, st, name, shape, dt):
        self.uid = getattr(self, "uid", 0) + 1
        return T(st.enter_context(self.nc.sbuf_tensor(f"{name}_{self.uid}", shape, dt)), name)

    def ps(self, st, name, shape, dt):
        self.uid = getattr(self, "uid", 0) + 1
        return T(st.enter_context(self.nc.psum_tensor(f"{name}_{self.uid}", shape, dt)), name, psum=True)


class Rot:
    def __init__(self, tiles):
        self.tiles = tiles
        self.i = 0

    def next(self):
        t = self.tiles[self.i % len(self.tiles)]
        self.i += 1
        return t


def build(n_layers=4, debug=False, mode="full"):
    nc = bass.Bass("TRN2", target_bir_lowering=False)
    dt_in = lambda n, s: nc.dram_tensor(n, s, F32, kind="ExternalInput")
    xT = dt_in("xT", [16, 128, 2048])
    selv_d = dt_in("selv", [128, 2])
    mixg_d = dt_in("mixg", [128, 64])
    mlpg_d = dt_in("mlpg", [128, 64])
    small = (mode != "full")
    wqkv = dt_in("wqkv", [1, 128, 128] if small else [2, 2048, 3072])
    qg_d = dt_in("qg", [2, 128])
    kg_d = dt_in("kg", [2, 128])
    lam_d = dt_in("lam", [2, 512])
    subln_d = dt_in("subln", [2, 256])
    wo = dt_in("wo", [1, 128, 128] if small else [2, 2048, 2048])
    win = dt_in("win", [2, 2048, 5376])
    convw_d = dt_in("convw", [2, 128, 96])
    convb_d = dt_in("convb", [2, 128, 24])
    dtb_d = dt_in("dtb", [2, 32])
    alog_d = dt_in("alog", [2, 32])
    dsk_d = dt_in("dsk", [2, 32])
    ngain_d = dt_in("ngain", [2, 2048])
    wout = dt_in("wout", [1, 128, 128] if small else [2, 4096, 2048])
    w1 = dt_in("w1", [1, 128, 128] if small else [n_layers, 2048, 8192])
    w2 = dt_in("w2", [1, 128, 128] if small else [n_layers, 8192, 2048])
    cos_d = dt_in("cos", [128, 512])
    sin_d = dt_in("sin", [128, 512])
    outT = nc.dram_tensor("outT", [16, 128, 2048], F32, kind="ExternalOutput")

    if debug:
        dbg_hn = nc.dram_tensor("dbg_hn", [4, 1024, 2048], BF16, kind="ExternalOutput")
        dbg_y = nc.dram_tensor("dbg_y", [8, 512, 4096], BF16, kind="ExternalOutput")
        dbg_mix = nc.dram_tensor("dbg_mix", [16, 128, 2048], F32, kind="ExternalOutput")
    res = nc.dram_tensor("res", [16, 128, 2048], F32)
    hn_own = nc.dram_tensor("hn_own", [4, 512, 2048], BF16)
    if mode == "ssm":
        hn_full = nc.dram_tensor("hn_in", [4, 1024, 2048], BF16, kind="ExternalInput")
        y_own = nc.dram_tensor("y_out", [8, 256, 4096], BF16, kind="ExternalOutput")
    else:
        hn_full = nc.dram_tensor("hn_full", [4, 1024, 2048], BF16)
        y_own = nc.dram_tensor("y_own", [8, 256, 4096], BF16)
    y_full = nc.dram_tensor("y_full", [8, 512, 4096], BF16)
    D_hn_own, D_hn_full, D_y_own, D_y_full = (T(None, n) for n in ("hn_own", "hn_full", "y_own", "y_full"))
    D_res = [T(None, f"res{i}") for i in range(2)]
    D_out = T(None, "out")

    with ExitStack() as gs:
        P = Prog(nc, gs)
        ident_bf = P.sb(gs, "ident_bf", [128, 128], BF16)
        ones_f = P.sb(gs, "ones_f", [128, 128], F32)
        tri_f = P.sb(gs, "tri_f", [128, 128], F32)
        tri_bf = P.sb(gs, "tri_bf", [128, 128], BF16)
        U_f = P.sb(gs, "U_f", [128, 128], F32)
        selv = P.sb(gs, "selv_sb", [128, 2], F32)
        mixg = P.sb(gs, "mixg_sb", [128, 64], F32)
        mlpg = P.sb(gs, "mlpg_sb", [128, 64], F32)

        P.op("pool", lambda e: e.memset(ones_f[:, :], 1.0), w=[ones_f])
        P.op("pool", lambda e: e.affine_select(out=tri_f[:, :], in_=ones_f[:, :], pattern=[[1, 128]],
                                                 compare_op=ALU.is_ge, fill=0.0, base=0, channel_multiplier=-1),
             r=[ones_f], w=[tri_f])
        P.op("pool", lambda e: e.affine_select(out=U_f[:, :], in_=ones_f[:, :], pattern=[[-1, 128]],
                                                 compare_op=ALU.is_ge, fill=0.0, base=-1, channel_multiplier=1),
             r=[ones_f], w=[U_f])
        P.op("pool", lambda e: e.affine_select(out=ident_bf[:, :], in_=ones_f[:, :], pattern=[[1, 128]],
                                                 compare_op=ALU.is_equal, fill=0.0, base=0, channel_multiplier=-1),
             r=[ones_f], w=[ident_bf])
        P.op("pool", lambda e: e.tensor_copy(out=tri_bf[:, :], in_=tri_f[:, :]), r=[tri_f], w=[tri_bf])
        P.dma("sp", selv[:, :], selv_d.ap(), "const", w=[selv])
        P.dma("sp", mixg[:, :], mixg_d.ap(), "const", w=[mixg])
        P.dma("sp", mlpg[:, :], mlpg_d.ap(), "const", w=[mlpg])
        P.end_phase()

        def load_w(slot, W2d, k0, n0, ncols):
            src = W2d[k0 * 128:(k0 + 16) * 128, n0:n0 + ncols].rearrange("(c p) n -> p c n", p=128)
            P.dma("pool", slot[:, :, 0:ncols], src, "d_" + slot.name, w=[slot])

        def allgather(src_t, dst_t, n, Dsrc, Ddst):
            for c in range(n):
                def fn(e, sem, c=c):
                    e.collective_compute("AllGather", ALU.bypass, replica_groups=GROUPS,
                                         ins=[src_t.ap()[c]], outs=[dst_t.ap()[c]]).then_inc(sem, 1)
                P.op("pool", fn, r=[Dsrc], w=[Ddst], dma="cc", inc=1)

        def phase_A(layer):
            src = xT if layer == 0 else res
            with ExitStack() as st:
                rt = [P.sb(st, f"A_rt{i}", [128, 16, 512], F32) for i in range(2)]
                sq = P.sb(st, "A_sq", [128, 16, 512], F32)
                rs = P.sb(st, "A_rs", [128, 512], F32)
                ri = P.sb(st, "A_ri", [128, 512], F32)
                hn = [P.sb(st, f"A_hn{i}", [128, 16, 512], BF16) for i in range(2)]
                ps = P.ps(st, "A_ps", [128, 512], F32)
                hview = hn_own.ap().rearrange("a (j p) t -> p (a j) t", p=128)
                for tt in range(4):
                    R = rt[tt % 2]
                    H = hn[tt % 2]
                    P.dma("sp", R[:, :, :], src.ap()[:, :, tt * 512:(tt + 1) * 512].rearrange("c p t -> p c t"),
                          "d_" + R.name, r=[D_res[tt // 2]], w=[R])
                    P.op("act", lambda e, R=R: e.activation(out=sq[:, :, :], in_=R[:, :, :], func=AF.Square),
                         r=[R], w=[sq])

                    def mm(e):
                        for c in range(16):
                            last = e.matmul(ps[:, :], lhsT=ones_f[:, :], rhs=sq[:, c, :], start=(c == 0), stop=(c == 15))
                        return last
                    P.op("pe", mm, r=[sq, ones_f], w=[ps])
                    P.op("act", lambda e: e.activation(out=rs[:, :], in_=ps[:, :], func=AF.Sqrt, bias=EPS_AP[:, 0:1],
                                                       scale=1.0 / 2048), r=[ps, EPS_T], w=[rs])
                    P.op("dve", lambda e: e.reciprocal(out=ri[:, :], in_=rs[:, :]), r=[rs], w=[ri])

                    def nrm(e, R=R, H=H):
                        for c in range(16):
                            last = e.scalar_tensor_tensor(out=H[:, c, :], in0=R[:, c, :],
                                                          scalar=mixg[:, layer * 16 + c:layer * 16 + c + 1],
                                                          in1=ri[:, :], op0=ALU.mult, op1=ALU.mult)
                        return last
                    P.op("dve", nrm, r=[R, ri, mixg], w=[H])
                    P.dma("sp", hview[:, :, tt * 512:(tt + 1) * 512], H[:, :, :], "d_" + H.name, r=[H], w=[D_hn_own])
                allgather(hn_own, hn_full, 4, D_hn_own, D_hn_full)
                if debug and layer == n_layers - 1:
                    P.dma("sp", dbg_hn.ap(), hn_full.ap(), "dbg", r=[D_hn_full])
                P.end_phase()

        def load_hn(HN, g):
            r, t0 = g // 4, (g % 4) * 512
            def fn(e, sem):
                for c in range(4):
                    e.dma_start(out=HN[:, c, :, :],
                                in_=hn_full.ap()[c, r * 512:(r + 1) * 512, t0:t0 + 512].rearrange("(j p) t -> p j t", p=128)
                                ).then_inc(sem, 16)
            P.op("sp", fn, r=[D_hn_full], w=[HN], dma="d_" + HN.name, inc=64)

        def phase_B_attn(layer):
            j = layer // 2
            linit = lambda_init(layer)
            with ExitStack() as st:
                QKT = P.sb(st, "QKT", [128, 4, 4096], BF16)
                Vaug = P.sb(st, "Vaug", [128, 32, 258], BF16)
                HNr = Rot([P.sb(st, f"HN{i}", [128, 4, 4, 512], BF16) for i in range(2)])
                wA = [P.sb(st, f"wA{i}", [128, 16, 512], BF16) for i in range(2)]
                wB = [P.sb(st, f"wB{i}", [128, 16, 256], BF16) for i in range(2)]
                G4 = P.sb(st, "G4", [128, 4, 128], F32)
                SG = P.sb(st, "SG", [128, 256], F32)
                COS = P.sb(st, "COS", [128, 32, 16], F32)
                SIN = P.sb(st, "SIN", [128, 32, 16], F32)
                row = P.sb(st, "row", [1, 1024], F32)
                sc = P.sb(st, "sc", [128, 8], F32)
                QK4r = Rot([P.sb(st, f"QK4_{i}", [128, 4, 512], F32) for i in range(2)])
                W4 = P.sb(st, "W4", [128, 4, 512], F32)
                s16 = P.sb(st, "s16", [128, 64], F32)
                rp = [P.sb(st, f"rp{i}", [128, 4, 4, 16], F32) for i in range(4)]
                QNb4r = Rot([P.sb(st, f"QNb4_{i}", [128, 4, 4, 128], BF16) for i in range(2)])
                pT = Rot([P.sb(st, f"pT{i}", [128, 512], BF16) for i in range(3)])
                O1n = P.sb(st, "O1n", [128, 4, 256], F32)
                ob = P.sb(st, "ob", [128, 256], F32)
                osq = P.sb(st, "osq", [128, 256], F32)
                onb = P.sb(st, "onb", [128, 256], BF16)
                e1 = P.sb(st, "e1", [128, 8], F32)
                oT = Rot([P.sb(st, f"oT{i}", [128, 2, 512], BF16) for i in range(2)])
                O = [P.ps(st, f"O{i}", [128, 512], F32) for i in range(4)]
                psr = Rot([P.ps(st, f"psr{i}", [128, 512], F32) for i in range(3)])
                psb = P.ps(st, "psb", [128, 8, 128], BF16)
                projrot = Rot(psr.tiles + O)

                bc = lambda d, n: d.ap()[j, :].partition_broadcast(128)
                P.dma("sp", G4[:, 0, :], bc(qg_d, 128), "const", w=[G4])
                P.dma("sp", G4[:, 1, :], bc(qg_d, 128), "const", w=[G4])
                P.dma("sp", G4[:, 2, :], bc(kg_d, 128), "const", w=[G4])
                P.dma("sp", G4[:, 3, :], bc(kg_d, 128), "const", w=[G4])
                P.dma("sp", SG[:, :], bc(subln_d, 256), "const", w=[SG])
                P.dma("sp", COS[:, :, :], cos_d.ap().rearrange("p (b f) -> p b f", f=16), "const", w=[COS])
                P.dma("sp", SIN[:, :, :], sin_d.ap().rearrange("p (b f) -> p b f", f=16), "const", w=[SIN])
                P.dma("sp", row[:, 0:512], lam_d.ap()[j:j + 1, :], "const", w=[row])
                P.dma("sp", row[:, 512:640], qg_d.ap()[j:j + 1, :], "const", w=[row])
                P.dma("sp", row[:, 640:768], kg_d.ap()[j:j + 1, :], "const", w=[row])
                P.op("act", lambda e: e.mul(out=G4[:, 0:2, :], in_=G4[:, 0:2, :], mul=128.0 ** -0.5), r=[G4], w=[G4])
                P.op("act", lambda e: e.mul(out=SG[:, :], in_=SG[:, :], mul=1.0 - linit), r=[SG], w=[SG])
                P.op("pool", lambda e: e.memset(Vaug[:, :, 256:258], 1.0), w=[Vaug])
                lv = row[:, 0:512].rearrange("p (a b d) -> p a b d", a=2, b=2)
                P.op("dve", lambda e: e.tensor_tensor(out=row[:, 768:1024].rearrange("p (a d) -> p a d", a=2),
                                                      in0=lv[:, :, 0, :], in1=lv[:, :, 1, :], op=ALU.mult), r=[row], w=[row])
                P.op("dve", lambda e: e.tensor_reduce(out=row[:, 0:2], in_=row[:, 768:1024].rearrange("p (a d) -> p a d", a=2),
                                                      axis=AX.X, op=ALU.add), r=[row], w=[row])
                P.op("act", lambda e: e.activation(out=row[:, 2:4], in_=row[:, 0:2], func=AF.Exp), r=[row], w=[row])
                P.op("dve", lambda e: e.tensor_tensor(out=row[:, 4:5], in0=row[:, 3:4], in1=row[:, 2:3], op=ALU.subtract),
                     r=[row], w=[row])
                P.op("dve", lambda e: e.tensor_scalar(out=row[:, 8:9], in0=row[:, 4:5], scalar1=-linit, scalar2=None,
                                                      op0=ALU.add), r=[row], w=[row])
                P.op("dve", lambda e: e.tensor_reduce(out=row[:, 5:7], in_=row[:, 512:768].rearrange("p (a d) -> p a d", a=2),
                                                      axis=AX.X, op=ALU.max, apply_absolute_value=True), r=[row], w=[row])
                P.op("dve", lambda e: e.tensor_tensor(out=row[:, 7:8], in0=row[:, 5:6], in1=row[:, 6:7], op=ALU.mult),
                     r=[row], w=[row])
                P.op("dve", lambda e: e.tensor_scalar(out=row[:, 9:10], in0=row[:, 7:8], scalar1=-(128.0 ** 0.5), scalar2=None,
                                                      op0=ALU.mult), r=[row], w=[row])
                pc = psr.next()
                P.op("pe", lambda e: e.matmul(pc[:, 0:2], lhsT=ones_f[0:1, :], rhs=row[:, 8:10], start=True, stop=True),
                     r=[row, ones_f], w=[pc])
                P.op("act", lambda e: e.copy(out=sc[:, 0:2], in_=pc[:, 0:2]), r=[pc], w=[sc])

                Wl = wqkv.ap()[j]
                for hd in range(4):
                    WA, WB = wA[hd % 2], wB[hd % 2]
                    load_w(WA, Wl, 0, hd * 768, 512)
                    load_w(WB, Wl, 0, hd * 768 + 512, 256)
                    def emit_tr(g, QN):
                        for rnd in range(2):
                            def tr(e, rnd=rnd, QN=QN):
                                for bl in range(2):
                                    for cc in range(4):
                                        last = e.transpose(out=psb[:, bl * 4 + cc, :], in_=QN[:, rnd * 2 + bl, cc, :], identity=ident_bf[:, :])
                                return last
                            P.op("pe", tr, r=[QN, ident_bf], w=[psb])
                            tok0 = (g * 4 + rnd * 2) * 128
                            P.op("act", lambda e, tok0=tok0: e.copy(
                                out=QKT[:, :, tok0:tok0 + 256].rearrange("p c (b t) -> p b c t", b=2),
                                in_=psb[:, :, :].rearrange("p (b c) t -> p b c t", b=2)), r=[psb], w=[QKT])
                    prev_tr = None
                    for g in range(8):
                        HN = HNr.next()
                        load_hn(HN, g)
                        QK4 = QK4r.next()
                        QNb4 = QNb4r.next()
                        for t4 in range(4):
                            tb = g * 4 + t4
                            pq, pv = projrot.next(), projrot.next()

                            def mmq(e, pq=pq, t4=t4, WA=WA, HN=HN):
                                for dc in range(16):
                                    last = e.matmul(pq[:, :], lhsT=HN[:, dc // 4, dc % 4, t4 * 128:(t4 + 1) * 128],
                                                    rhs=WA[:, dc, :], start=(dc == 0), stop=(dc == 15))
                                return last
                            P.op("pe", mmq, r=[HN, WA], w=[pq])

                            def mmv(e, pv=pv, t4=t4, WB=WB, HN=HN):
                                for dc in range(16):
                                    last = e.matmul(pv[:, 0:256], lhsT=HN[:, dc // 4, dc % 4, t4 * 128:(t4 + 1) * 128],
                                                    rhs=WB[:, dc, :], start=(dc == 0), stop=(dc == 15))
                                return last
                            P.op("pe", mmv, r=[HN, WB], w=[pv])
                            P.op("act", lambda e, pv=pv, tb=tb: e.copy(out=Vaug[:, tb, 0:256], in_=pv[:, 0:256]), r=[pv], w=[Vaug])
                            P.op("act", lambda e, pq=pq, t4=t4, QK4=QK4: e.copy(out=QK4[:, t4, :], in_=pq[:, :]), r=[pq], w=[QK4])
                        if prev_tr is not None:
                            emit_tr(*prev_tr)
                        f16 = lambda T_: T_[:, :, :].rearrange("p b (a d) -> p (b a) d", d=128)
                        f4 = lambda T_: T_[:, :, :].rearrange("p b (a d) -> p b a d", d=128)
                        P.op("act", lambda e, QK4=QK4: e.activation(out=W4[:, :, :], in_=QK4[:, :, :], func=AF.Square), r=[QK4], w=[W4])
                        P.op("dve", lambda e: e.tensor_reduce(out=s16[:, 0:16], in_=f16(W4), axis=AX.X, op=ALU.add), r=[W4], w=[s16])
                        P.op("dve", lambda e: e.tensor_scalar(out=s16[:, 16:32], in0=s16[:, 0:16], scalar1=1.0 / 128, scalar2=EPS,
                                                              op0=ALU.mult, op1=ALU.add), r=[s16], w=[s16])
                        P.op("act", lambda e: e.activation(out=s16[:, 32:48], in_=s16[:, 16:32], func=AF.Sqrt), r=[s16], w=[s16])
                        P.op("dve", lambda e: e.reciprocal(out=s16[:, 48:64], in_=s16[:, 32:48]), r=[s16], w=[s16])
                        P.op("dve", lambda e, QK4=QK4: e.tensor_tensor(out=f16(W4), in0=f16(QK4),
                                                                       in1=s16[:, 48:64].unsqueeze(2).broadcast_to([128, 16, 128]), op=ALU.mult),
                             r=[QK4, s16], w=[W4])
                        P.op("dve", lambda e: e.tensor_tensor(out=f4(W4), in0=f4(W4),
                                                              in1=G4[:, :, :].unsqueeze(1).broadcast_to([128, 4, 4, 128]), op=ALU.mult),
                             r=[W4, G4], w=[W4])
                        cb = lambda g=g: COS[:, g * 4:(g + 1) * 4, :].unsqueeze(2).broadcast_to([128, 4, 4, 16])
                        sb_ = lambda g=g: SIN[:, g * 4:(g + 1) * 4, :].unsqueeze(2).broadcast_to([128, 4, 4, 16])
                        P.op("dve", lambda e, cb=cb: e.tensor_tensor(out=rp[0][:, :, :, :], in0=f4(W4)[:, :, :, 0:16], in1=cb(), op=ALU.mult),
                             r=[W4, COS], w=[rp[0]])
                        P.op("dve", lambda e, sb_=sb_: e.tensor_tensor(out=rp[1][:, :, :, :], in0=f4(W4)[:, :, :, 16:32], in1=sb_(), op=ALU.mult),
                             r=[W4, SIN], w=[rp[1]])
                        P.op("dve", lambda e, cb=cb: e.tensor_tensor(out=rp[2][:, :, :, :], in0=f4(W4)[:, :, :, 16:32], in1=cb(), op=ALU.mult),
                             r=[W4, COS], w=[rp[2]])
                        P.op("dve", lambda e, sb_=sb_: e.tensor_tensor(out=rp[3][:, :, :, :], in0=f4(W4)[:, :, :, 0:16], in1=sb_(), op=ALU.mult),
                             r=[W4, SIN], w=[rp[3]])
                        P.op("dve", lambda e, QNb4=QNb4: e.tensor_tensor(out=QNb4[:, :, :, 0:16], in0=rp[0][:, :, :, :], in1=rp[1][:, :, :, :], op=ALU.subtract),
                             r=[rp[0], rp[1]], w=[QNb4])
                        P.op("dve", lambda e, QNb4=QNb4: e.tensor_tensor(out=QNb4[:, :, :, 16:32], in0=rp[2][:, :, :, :], in1=rp[3][:, :, :, :], op=ALU.add),
                             r=[rp[2], rp[3]], w=[QNb4])
                        P.op("act", lambda e, QNb4=QNb4: e.copy(out=QNb4[:, :, :, 32:128], in_=f4(W4)[:, :, :, 32:128]), r=[W4], w=[QNb4])
                        prev_tr = (g, QNb4)
                    emit_tr(*prev_tr)
                    for jq in range(8):
                        q0 = jq * 512
                        OT = oT.next()
                        for c in range(2):
                            nkb = 4 * jq + 4
                            sps = {}

                            def issue_S(kb, c=c, jq=jq, q0=q0):
                                r0 = max(0, kb - 4 * jq)
                                off = r0 * 128
                                sp_ = psr.next()
                                P.op("pe", lambda e, sp_=sp_, kb=kb, c=c, off=off, q0=q0: e.matmul(
                                    sp_[:, off:512], lhsT=QKT[:, 2 + c, kb * 128:(kb + 1) * 128],
                                    rhs=QKT[:, c, q0 + off:q0 + 512], start=True, stop=True), r=[QKT], w=[sp_])
                                sps[kb] = (sp_, off, r0)
                            for kb in range(min(2, nkb)):
                                issue_S(kb)
                            for kb in range(nkb):
                                if kb + 2 < nkb:
                                    issue_S(kb + 2)
                                sp_, off, r0 = sps.pop(kb)
                                pt = pT.next()
                                P.op("act", lambda e, sp_=sp_, pt=pt, off=off: e.activation(
                                    out=pt[:, off:512], in_=sp_[:, off:512], func=AF.Exp, bias=sc[:, 1:2], scale=1.0),
                                    r=[sp_, sc], w=[pt])
                                if kb >= 4 * jq:
                                    P.op("dve", lambda e, pt=pt, off=off: e.tensor_tensor(
                                        out=pt[:, off:off + 128], in0=pt[:, off:off + 128], in1=tri_bf[:, :], op=ALU.mult),
                                        r=[pt, tri_bf], w=[pt])

                                def pv_(e, pt=pt, kb=kb, r0=r0, jq=jq):
                                    for qb in range(r0, 4):
                                        last = e.matmul(O[qb][:, 0:257], lhsT=pt[:, qb * 128:(qb + 1) * 128],
                                                        rhs=Vaug[:, kb, 0:257], start=(kb == 0), stop=(kb == 4 * jq + qb))
                                    return last
                                P.op("pe", pv_, r=[pt, Vaug], w=[O[qb] for qb in range(r0, 4)])
                            for qb in range(4):
                                Oq = O[qb]
                                P.op("dve", lambda e, Oq=Oq: e.reciprocal(out=e1[:, 0:1], in_=Oq[:, 256:257]), r=[Oq], w=[e1])
                                if c == 0:
                                    P.op("dve", lambda e, Oq=Oq, qb=qb: e.tensor_scalar(
                                        out=O1n[:, qb, :], in0=Oq[:, 0:256], scalar1=e1[:, 0:1], scalar2=None, op0=ALU.mult),
                                        r=[Oq, e1], w=[O1n])
                                else:
                                    P.op("dve", lambda e: e.tensor_tensor(out=e1[:, 1:2], in0=e1[:, 0:1], in1=sc[:, 0:1], op=ALU.mult),
                                         r=[e1, sc], w=[e1])
                                    P.op("dve", lambda e, Oq=Oq, qb=qb: e.scalar_tensor_tensor(
                                        out=ob[:, :], in0=Oq[:, 0:256], scalar=e1[:, 1:2], in1=O1n[:, qb, :],
                                        op0=ALU.mult, op1=ALU.add), r=[Oq, e1, O1n], w=[ob])
                                    P.op("act", lambda e: e.activation(out=osq[:, :], in_=ob[:, :], func=AF.Square,
                                                                       accum_out=e1[:, 2:3]), r=[ob], w=[osq, e1])
                                    P.op("dve", lambda e: e.tensor_scalar(out=e1[:, 3:4], in0=e1[:, 2:3], scalar1=1.0 / 256,
                                                                          scalar2=EPS, op0=ALU.mult, op1=ALU.add), r=[e1], w=[e1])
                                    P.op("act", lambda e: e.activation(out=e1[:, 4:5], in_=e1[:, 3:4], func=AF.Sqrt), r=[e1], w=[e1])
                                    P.op("dve", lambda e: e.reciprocal(out=e1[:, 5:6], in_=e1[:, 4:5]), r=[e1], w=[e1])
                                    P.op("dve", lambda e: e.scalar_tensor_tensor(
                                        out=onb[:, :], in0=ob[:, :], scalar=e1[:, 5:6], in1=SG[:, :], op0=ALU.mult, op1=ALU.mult),
                                        r=[ob, e1, SG], w=[onb])

                                    def tr2(e):
                                        for i in range(2):
                                            last = e.transpose(out=psb[:, i, :], in_=onb[:, i * 128:(i + 1) * 128], identity=ident_bf[:, :])
                                        return last
                                    P.op("pe", tr2, r=[onb, ident_bf], w=[psb])
                                    P.op("act", lambda e, OT=OT, qb=qb: e.copy(out=OT[:, :, qb * 128:(qb + 1) * 128], in_=psb[:, 0:2, :]),
                                         r=[psb], w=[OT])
                        yv = y_own.ap().rearrange("a b t -> (a b) t")[hd * 256:(hd + 1) * 256, q0:q0 + 512].rearrange("(i p) t -> p i t", p=128)
                        P.dma("sp", yv, OT[:, :, :], "d_" + OT.name, r=[OT], w=[D_y_own])
                allgather(y_own, y_full, 4, D_y_own, D_y_full)
                if debug and layer == n_layers - 1:
                    P.dma("sp", dbg_y.ap(), y_full.ap(), "dbg", r=[D_y_full])
                P.end_phase()

        def phase_B_ssm(layer):
            j = layer // 2
            with ExitStack() as st:
                sHNr = Rot([P.sb(st, f"sHN{i}", [128, 4, 4, 512], BF16) for i in range(2)])
                wsl = Rot([P.sb(st, f"sw{i}", [128, 16, 256], BF16) for i in range(3)])
                CW = P.sb(st, "CW", [128, 24, 4], F32)
                CB_ = P.sb(st, "CBias", [128, 24], F32)
                DTB = P.sb(st, "DTB", [128, 32], F32)
                AB = P.sb(st, "AB", [128, 32], F32)
                DB = P.sb(st, "DB", [128, 32], F32)
                NG = P.sb(st, "NG", [128, 2048], F32)
                HS = P.sb(st, "HS", [128, 24, 4], F32)
                Ee = Rot([P.sb(st, f"Ee{i}", [128, 516], F32) for i in range(2)])
                acc = Rot([P.sb(st, f"acc{i}", [128, 512], F32) for i in range(2)])
                XBC = P.sb(st, "XBC", [128, 24, 512], BF16)
                ZS = P.sb(st, "ZS", [128, 4, 2048], BF16)
                DTr = P.sb(st, "DTr", [128, 4, 32], F32)
                d8 = P.sb(st, "d8", [128, 8, 32], F32)
                Xg = Rot([P.sb(st, f"Xg{i}", [128, 8, 128], F32) for i in range(2)])
                xtok = P.sb(st, "xtok", [128, 2048], BF16)
                xdt = P.sb(st, "xdt", [128, 2048], BF16)
                xdtd = P.sb(st, "xdtd", [128, 2048], BF16)
                Btok = P.sb(st, "Btok", [128, 4, 128], BF16)
                CBm4 = [P.sb(st, f"CBm{i}", [128, 128], F32) for i in range(4)]
                Eg = Rot([P.sb(st, f"Eg{i}", [128, 4, 128], F32) for i in range(4)])
                Mt4 = [P.sb(st, f"Mt{i}", [128, 8, 128], BF16) for i in range(4)]
                deferred = [None]
                Y = P.sb(st, "Y", [128, 512], F32)
                t1 = P.sb(st, "t1", [128, 512], F32)
                t3 = P.sb(st, "t3", [128, 512], F32)
                ysq = P.sb(st, "ysq", [128, 512], F32)
                ynb4 = [P.sb(st, f"ynb{i}", [128, 512], BF16) for i in range(4)]
                g4 = P.sb(st, "g4", [128, 8], F32)
                stf = P.sb(st, "stf", [128, 2048], F32)
                stb = P.sb(st, "stb", [128, 2048], BF16)
                yT = Rot([P.sb(st, f"yT{i}", [128, 16, 512], BF16) for i in range(1)])
                psr = Rot([P.ps(st, f"spsr{i}", [128, 512], F32) for i in range(6)])
                psb = Rot([P.ps(st, f"spsb{i}", [128, 8, 128], BF16) for i in range(2)])

                bc = lambda d: d.ap()[j, :].partition_broadcast(128)
                P.dma("sp", CW[:, :, :], convw_d.ap()[j].rearrange("p (c k) -> p c k", k=4), "const", w=[CW])
                P.dma("sp", CB_[:, :], convb_d.ap()[j], "const", w=[CB_])
                P.dma("sp", DTB[:, :], bc(dtb_d), "const", w=[DTB])
                P.dma("sp", AB[:, :], bc(alog_d), "const", w=[AB])
                P.dma("sp", DB[:, :], bc(dsk_d), "const", w=[DB])
                P.dma("sp", NG[:, :], bc(ngain_d), "const", w=[NG])
                P.op("act", lambda e: e.activation(out=AB[:, :], in_=AB[:, :], func=AF.Exp), r=[AB], w=[AB])
                P.op("act", lambda e: e.mul(out=AB[:, :], in_=AB[:, :], mul=-1.0), r=[AB], w=[AB])
                P.op("pool", lambda e: e.memset(HS[:, :, :], 0.0), w=[HS])
                P.op("pool", lambda e: e.memset(stf[:, :], 0.0), w=[stf])
                P.op("pool", lambda e: e.memset(stb[:, :], 0.0), w=[stb])
                Wl = win.ap()[j]

                try:
                  chk(1)
                  for g in range(8):
                      HN = sHNr.next()
                      load_hn(HN, g)
                      YT = yT.next()
                      hnr = lambda dc, HN=HN: HN[:, dc // 4, dc % 4, :]
                      for cp in range(12):
                          W = wsl.next()
                          load_w(W, Wl, 0, cp * 256, 256)
                          for cl in range(2):
                              ch = cp * 2 + cl
                              pp = psr.next()

                              def mmx(e, pp=pp, W=W, cl=cl, hnr=hnr):
                                  for dc in range(16):
                                      last = e.matmul(pp[:, :], lhsT=W[:, dc, cl * 128:(cl + 1) * 128], rhs=hnr(dc),
                                                      start=(dc == 0), stop=(dc == 15))
                                  return last
                              P.op("pe", mmx, r=[HN, W], w=[pp])
                              chk(2)
                              E = Ee.next()
                              A_ = acc.next()
                              P.op("act", lambda e, E=E, ch=ch: e.copy(out=E[:, 0:3], in_=HS[:, ch, 0:3]), r=[HS], w=[E])
                              P.op("act", lambda e, E=E, pp=pp: e.copy(out=E[:, 3:515], in_=pp[:, :]), r=[pp], w=[E])
                              P.op("act", lambda e, E=E, ch=ch: e.copy(out=HS[:, ch, 0:3], in_=E[:, 512:515]), r=[E], w=[HS])
                              P.op("dve", lambda e, E=E, A_=A_, ch=ch: e.tensor_scalar(
                                  out=A_[:, :], in0=E[:, 0:512], scalar1=CW[:, ch, 0:1], scalar2=None, op0=ALU.mult),
                                  r=[E, CW], w=[A_])
                              for k in range(1, 4):
                                  P.op("dve", lambda e, E=E, A_=A_, ch=ch, k=k: e.scalar_tensor_tensor(
                                      out=A_[:, :], in0=E[:, k:k + 512], scalar=CW[:, ch, k:k + 1], in1=A_[:, :],
                                      op0=ALU.mult, op1=ALU.add), r=[E, CW, A_], w=[A_])
                              P.op("act", lambda e, A_=A_, ch=ch: e.activation(out=XBC[:, ch, :], in_=A_[:, :], func=AF.Silu,
                                                                              bias=CB_[:, ch:ch + 1], scale=1.0), r=[A_, CB_], w=[XBC])
                      chk(3)
                      for zc in range(8):
                          W = wsl.next()
                          load_w(W, Wl, 0, 3072 + zc * 256, 256)
                          for t4 in range(4):
                              pp = psr.next()

                              def mmz(e, pp=pp, W=W, t4=t4, HN=HN):
                                  for dc in range(16):
                                      last = e.matmul(pp[:, 0:256], lhsT=HN[:, dc // 4, dc % 4, t4 * 128:(t4 + 1) * 128],
                                                      rhs=W[:, dc, :], start=(dc == 0), stop=(dc == 15))
                                  return last
                              P.op("pe", mmz, r=[HN, W], w=[pp])
                              P.op("act", lambda e, pp=pp, t4=t4, zc=zc: e.activation(
                                  out=ZS[:, t4, zc * 256:(zc + 1) * 256], in_=pp[:, 0:256], func=AF.Silu), r=[pp], w=[ZS])
                      wdt = wsl.next()
                      load_w(wdt, Wl, 0, 5120, 256)
                      for t4 in range(4):
                          pp = psr.next()

                          def mmd(e, pp=pp, t4=t4, wdt=wdt, HN=HN):
                              for dc in range(16):
                                  last = e.matmul(pp[:, 0:32], lhsT=HN[:, dc // 4, dc % 4, t4 * 128:(t4 + 1) * 128],
                                                  rhs=wdt[:, dc, 0:32], start=(dc == 0), stop=(dc == 15))
                              return last
                          P.op("pe", mmd, r=[HN, wdt], w=[pp])
                          P.op("dve", lambda e, pp=pp, t4=t4: e.tensor_tensor(out=DTr[:, t4, :], in0=pp[:, 0:32], in1=DTB[:, :], op=ALU.add),
                               r=[pp, DTB], w=[DTr])
                      chk(4)
                      for t4 in range(4):
                          ts_ = slice(t4 * 128, (t4 + 1) * 128)
                          xx = DTr[:, t4, :]
                          P.op("act", lambda e, xx=xx: e.activation(out=d8[:, 0, :], in_=xx, func=AF.Abs), r=[DTr], w=[d8])
                          P.op("act", lambda e: e.activation(out=d8[:, 1, :], in_=d8[:, 0, :], func=AF.Exp, scale=-1.0), r=[d8], w=[d8])
                          P.op("act", lambda e: e.activation(out=d8[:, 2, :], in_=d8[:, 1, :], func=AF.Ln, bias=EPS_T[:, 1:2], scale=1.0),
                               r=[d8, EPS_T], w=[d8])
                          P.op("dve", lambda e, xx=xx: e.scalar_tensor_tensor(out=d8[:, 3, :], in0=xx, scalar=0.0, in1=d8[:, 2, :],
                                                                              op0=ALU.max, op1=ALU.add), r=[DTr, d8], w=[d8])
                          P.op("dve", lambda e: e.tensor_tensor(out=d8[:, 4, :], in0=d8[:, 3, :], in1=AB[:, :], op=ALU.mult),
                               r=[d8, AB], w=[d8])
                          pa = psr.next()
                          P.op("pe", lambda e, pa=pa: e.matmul(pa[:, 0:32], lhsT=tri_f[:, :], rhs=d8[:, 4, :], start=True, stop=True),
                               r=[d8, tri_f], w=[pa])
                          P.op("pe", lambda e, pa=pa: e.matmul(pa[:, 32:64], lhsT=ones_f[:, :], rhs=d8[:, 4, :], start=True, stop=True),
                               r=[d8, ones_f], w=[pa])
                          P.op("act", lambda e, pa=pa: e.copy(out=d8[:, 5, :], in_=pa[:, 0:32]), r=[pa], w=[d8])
                          P.op("act", lambda e: e.activation(out=d8[:, 6, :], in_=d8[:, 5, :], func=AF.Exp), r=[d8], w=[d8])
                          P.op("act", lambda e, pa=pa: e.activation(out=d8[:, 7, :], in_=pa[:, 32:64], func=AF.Exp), r=[pa], w=[d8])
                          P.op("dve", lambda e, pa=pa: e.tensor_tensor(out=d8[:, 0, :], in0=pa[:, 32:64], in1=d8[:, 5, :], op=ALU.subtract),
                               r=[pa, d8], w=[d8])
                          P.op("act", lambda e: e.activation(out=d8[:, 1, :], in_=d8[:, 0, :], func=AF.Exp), r=[d8], w=[d8])
                          chk(5)
                          for half in range(2):
                              pb = psb.next()

                              def trx(e, pb=pb, half=half, ts_=ts_):
                                  for i in range(8):
                                      last = e.transpose(out=pb[:, i, :], in_=XBC[:, half * 8 + i, ts_], identity=ident_bf[:, :])
                                  return last
                              P.op("pe", trx, r=[XBC, ident_bf], w=[pb])
                              hs = slice(half * 1024, (half + 1) * 1024)
                              P.op("act", lambda e, pb=pb, hs=hs: e.copy(out=xtok[:, hs], in_=pb[:, :, :].rearrange("p a b -> p (a b)")),
                                   r=[pb], w=[xtok])
                              P.op("dve", lambda e, pb=pb, hs=hs, half=half: e.tensor_tensor(
                                  out=xdt[:, hs].rearrange("p (h d) -> p h d", d=64),
                                  in0=pb[:, :, :].rearrange("p a (b d) -> p (a b) d", d=64),
                                  in1=d8[:, 3, half * 16:(half + 1) * 16].unsqueeze(2).broadcast_to([128, 16, 64]), op=ALU.mult),
                                  r=[pb, d8], w=[xdt])
                          P.op("dve", lambda e: e.tensor_tensor(
                              out=xdtd[:, :].rearrange("p (h d) -> p h d", d=64), in0=xdt[:, :].rearrange("p (h d) -> p h d", d=64),
                              in1=d8[:, 1, :].unsqueeze(2).broadcast_to([128, 32, 64]), op=ALU.mult), r=[xdt, d8], w=[xdtd])
                          pb = psb.next()

                          def trb(e, pb=pb, ts_=ts_):
                              for i in range(4):
                                  last = e.transpose(out=pb[:, i, :], in_=XBC[:, 16 + i, ts_], identity=ident_bf[:, :])
                              return last
                          P.op("pe", trb, r=[XBC, ident_bf], w=[pb])
                          P.op("act", lambda e, pb=pb: e.copy(out=Btok[:, :, :], in_=pb[:, 0:4, :]), r=[pb], w=[Btok])
                          chk(6)
                          v3 = lambda ap: ap.rearrange("p (h d) -> p h d", d=64)
                          pend = None

                          def emit_mt(pd):
                              gg_, egs = pd
                              for q, EG in enumerate(egs):
                                  P.op("dve", lambda e, EG=EG, q=q, gg_=gg_: e.tensor_tensor(
                                      out=Mt4[gg_][:, q * 4:(q + 1) * 4, :], in0=EG[:, :, :],
                                      in1=CBm4[gg_][:, :].unsqueeze(1).broadcast_to([128, 4, 128]), op=ALU.mult),
                                      r=[EG, CBm4[gg_]], w=[Mt4[gg_]])
                          for gg in range(4):
                              pcb = psr.next()
                              P.op("pe", lambda e, pcb=pcb, gg=gg, ts_=ts_: e.matmul(
                                  pcb[:, 0:128], lhsT=XBC[:, 16 + gg, ts_], rhs=XBC[:, 20 + gg, ts_], start=True, stop=True),
                                  r=[XBC], w=[pcb])
                              P.op("dve", lambda e, pcb=pcb, gg=gg: e.tensor_tensor(out=CBm4[gg][:, :], in0=pcb[:, 0:128], in1=tri_f[:, :], op=ALU.mult),
                                   r=[pcb, tri_f], w=[CBm4[gg]])
                              X = Xg.next()
                              P.op("dve", lambda e, X=X, gg=gg: e.tensor_tensor(
                                  out=X[:, :, :], in0=tri_f[:, :].unsqueeze(1).broadcast_to([128, 8, 128]),
                                  in1=d8[:, 4, gg * 8:(gg + 1) * 8].unsqueeze(2).broadcast_to([128, 8, 128]), op=ALU.mult),
                                  r=[tri_f, d8], w=[X])
                              egs = []
                              for q in range(2):
                                  psg = psr.next()
                                  P.op("pe", lambda e, psg=psg, X=X, q=q: e.matmul(
                                      psg[:, :], lhsT=U_f[:, :], rhs=X[:, q * 4:(q + 1) * 4, :].rearrange("p a b -> p (a b)"),
                                      start=True, stop=True), r=[X, U_f], w=[psg])
                                  EG = Eg.next()
                                  P.op("act", lambda e, psg=psg, EG=EG: e.activation(
                                      out=EG[:, :, :].rearrange("p a b -> p (a b)"), in_=psg[:, :], func=AF.Exp), r=[psg], w=[EG])
                                  egs.append(EG)
                              if pend is not None:
                                  emit_mt(pend)
                              pend = (gg, egs)
                          emit_mt(pend)
                          if deferred[0] is not None:
                              deferred[0]()
                              deferred[0] = None
                          chk(7)
                          for gg in range(4):
                              gs_ = slice(gg * 512, (gg + 1) * 512)
                              pyd = psr.next()

                              def mmy(e, pyd=pyd, gg=gg):
                                  for hh in range(8):
                                      h = gg * 8 + hh
                                      last = e.matmul(pyd[:, hh * 64:(hh + 1) * 64], lhsT=Mt4[gg][:, hh, :], rhs=xdt[:, h * 64:(h + 1) * 64],
                                                      start=True, stop=True)
                                  return last
                              P.op("pe", mmy, r=[Mt4[gg], xdt], w=[pyd])
                              pcs = psr.next()
                              P.op("pe", lambda e, pcs=pcs, gg=gg, ts_=ts_, gs_=gs_: e.matmul(
                                  pcs[:, :], lhsT=XBC[:, 20 + gg, ts_], rhs=stb[:, gs_], start=True, stop=True), r=[XBC, stb], w=[pcs])
                              pds = psr.next()
                              P.op("pe", lambda e, pds=pds, gg=gg, gs_=gs_: e.matmul(
                                  pds[:, :], lhsT=Btok[:, gg, :], rhs=xdtd[:, gs_], start=True, stop=True), r=[Btok, xdtd], w=[pds])
                              b8 = lambda row, gg=gg: d8[:, row, gg * 8:(gg + 1) * 8].unsqueeze(2).broadcast_to([128, 8, 64])
                              P.op("dve", lambda e, pcs=pcs, b8=b8: e.tensor_tensor(out=v3(t1[:, :]), in0=v3(pcs[:, :]), in1=b8(6), op=ALU.mult),
                                   r=[pcs, d8], w=[t1])
                              P.op("dve", lambda e, pyd=pyd: e.tensor_tensor(out=Y[:, :], in0=pyd[:, :], in1=t1[:, :], op=ALU.add),
                                   r=[pyd, t1], w=[Y])
                              P.op("dve", lambda e, gs_=gs_, b8=b8: e.tensor_tensor(out=v3(t1[:, :]), in0=v3(stf[:, gs_]), in1=b8(7), op=ALU.mult),
                                   r=[stf, d8, Y], w=[t1])
                              P.op("dve", lambda e, pds=pds, gs_=gs_: e.tensor_tensor(out=stf[:, gs_], in0=pds[:, :], in1=t1[:, :], op=ALU.add),
                                   r=[pds, t1], w=[stf])
                              P.op("act", lambda e, gs_=gs_: e.copy(out=stb[:, gs_], in_=stf[:, gs_]), r=[stf], w=[stb])
                              chk(8)
                              P.op("dve", lambda e, gs_=gs_, gg=gg: e.tensor_tensor(
                                  out=v3(t3[:, :]), in0=v3(xtok[:, gs_]),
                                  in1=DB[:, gg * 8:(gg + 1) * 8].unsqueeze(2).broadcast_to([128, 8, 64]), op=ALU.mult), r=[xtok, DB], w=[t3])
                              P.op("dve", lambda e: e.tensor_tensor(out=Y[:, :], in0=Y[:, :], in1=t3[:, :], op=ALU.add), r=[Y, t3], w=[Y])
                              P.op("dve", lambda e, t4=t4, gs_=gs_: e.tensor_tensor(out=Y[:, :], in0=Y[:, :], in1=ZS[:, t4, gs_], op=ALU.mult),
                                   r=[Y, ZS], w=[Y])
                              P.op("act", lambda e: e.activation(out=ysq[:, :], in_=Y[:, :], func=AF.Square, accum_out=g4[:, 0:1]),
                                   r=[Y], w=[ysq, g4])
                              P.op("dve", lambda e: e.tensor_scalar(out=g4[:, 1:2], in0=g4[:, 0:1], scalar1=1.0 / 512, scalar2=EPS,
                                                                    op0=ALU.mult, op1=ALU.add), r=[g4], w=[g4])
                              P.op("act", lambda e: e.activation(out=g4[:, 2:3], in_=g4[:, 1:2], func=AF.Sqrt), r=[g4], w=[g4])
                              P.op("dve", lambda e: e.reciprocal(out=g4[:, 3:4], in_=g4[:, 2:3]), r=[g4], w=[g4])
                              P.op("dve", lambda e, gs_=gs_, gg=gg: e.scalar_tensor_tensor(
                                  out=ynb4[gg][:, :], in0=Y[:, :], scalar=g4[:, 3:4], in1=NG[:, gs_], op0=ALU.mult, op1=ALU.mult),
                                  r=[Y, g4, NG], w=[ynb4[gg]])

                          def sweep3(YT=YT, ts_=ts_):
                              for gg in range(4):
                                  pb = psb.next()

                                  def try_(e, pb=pb, gg=gg):
                                      for i in range(4):
                                          last = e.transpose(out=pb[:, i, :], in_=ynb4[gg][:, i * 128:(i + 1) * 128], identity=ident_bf[:, :])
                                      return last
                                  P.op("pe", try_, r=[ynb4[gg], ident_bf], w=[pb])
                                  P.op("act", lambda e, pb=pb, YT=YT, gg=gg, ts_=ts_: e.copy(out=YT[:, gg * 4:(gg + 1) * 4, ts_], in_=pb[:, 0:4, :]),
                                       r=[pb], w=[YT])
                          deferred[0] = sweep3
                      if deferred[0] is not None:
                          deferred[0]()
                          deferred[0] = None
                      chk(9)
                      yv = y_own.ap().rearrange("a b t -> (a b) t")[:, g * 512:(g + 1) * 512].rearrange("(c p) t -> p c t", p=128)
                      P.dma("sp", yv, YT[:, :, :], "d_" + YT.name, r=[YT], w=[D_y_own])
                except _Stop:
                    pass
                if mode == "full":
                    allgather(y_own, y_full, 8, D_y_own, D_y_full)
                if debug and layer == n_layers - 1:
                    P.dma("sp", dbg_y.ap(), y_full.ap(), "dbg", r=[D_y_full])
                P.end_phase()

        def phase_C(layer, tt2):
            j = layer // 2
            attn = (layer % 2 == 0)
            KC = 16 if attn else 32
            ncc = 4 if attn else 8
            Wo = wo.ap()[j] if attn else wout.ap()[j]
            src = xT if layer == 0 else res
            last = (layer == n_layers - 1)
            dst = outT if last else res
            t0 = tt2 * 1024
            with ExitStack() as so:
                hn2 = P.sb(so, "hn2", [128, 16, 1024], BF16)
                with ExitStack() as st:
                    R = P.sb(st, "C_R", [128, 16, 1024], F32)
                    yT = P.sb(st, "C_yT", [128, KC, 1024], BF16)
                    lo = Rot([P.sb(st, f"C_lo{i}", [128, 2, 1024], BF16) for i in range(2)])
                    hi = Rot([P.sb(st, f"C_hi{i}", [128, 2, 1024], BF16) for i in range(2)])
                    wsl = Rot([P.sb(st, f"C_w{i}", [128, 16, 256], BF16) for i in range(2)])
                    sq = Rot([P.sb(st, f"C_sq{i}", [128, 1024], F32) for i in range(1)])
                    rs = P.sb(st, "C_rs", [128, 1024], F32)
                    ri = rs
                    pp = [[P.ps(st, f"C_p{a}{b}", [128, 512], F32) for b in range(2)] for a in range(2)]
                    pn = [P.ps(st, f"C_pn{b}", [128, 512], F32) for b in range(2)]
                    P.dma("sp", R[:, :, :], src.ap()[:, :, t0:t0 + 1024].rearrange("c p t -> p c t"), "d_C_R", r=[D_res[tt2]], w=[R])
                    for r in range(2):
                        for c in range(ncc):
                            L, H = lo.next(), hi.next()
                            rows = slice(r * 256, (r + 1) * 256)
                            P.dma("sp", L[:, :, :], y_full.ap()[c, rows, t0:t0 + 1024].rearrange("(h p) t -> p h t", p=128),
                                  "d_" + L.name, r=[D_y_full], w=[L])
                            P.dma("sp", H[:, :, :], y_full.ap()[c, rows, 2048 + t0:2048 + t0 + 1024].rearrange("(h p) t -> p h t", p=128),
                                  "d_" + H.name, r=[D_y_full], w=[H])
                            kc0 = r * (KC // 2) + c * 2
                            P.op("dve", lambda e, H=H: e.tensor_scalar(out=H[:, :, :], in0=H[:, :, :], scalar1=selv[:, 0:1], scalar2=None,
                                                                      op0=ALU.mult), r=[H, selv], w=[H])
                            P.op("dve", lambda e, L=L, H=H, kc0=kc0: e.scalar_tensor_tensor(
                                out=yT[:, kc0:kc0 + 2, :], in0=L[:, :, :], scalar=selv[:, 1:2], in1=H[:, :, :], op0=ALU.mult, op1=ALU.add),
                                r=[L, selv, H], w=[yT])
                    NKQ = KC // 16
                    for dcp in range(8):
                        for kq in range(NKQ):
                            W = wsl.next()
                            load_w(W, Wo, kq * 16, dcp * 256, 256)

                            def mmo(e, kq=kq, W=W):
                                for dcl in range(2):
                                    for sub in range(2):
                                        for c16 in range(16):
                                            last_ = e.matmul(pp[dcl][sub][:, :], lhsT=W[:, c16, dcl * 128:(dcl + 1) * 128],
                                                             rhs=yT[:, kq * 16 + c16, sub * 512:(sub + 1) * 512],
                                                             start=(kq == 0 and c16 == 0), stop=(kq == NKQ - 1 and c16 == 15))
                                return last_
                            P.op("pe", mmo, r=[yT, W], w=[pp[0][0], pp[0][1], pp[1][0], pp[1][1]])
                        for dcl in range(2):
                            for sub in range(2):
                                dc = dcp * 2 + dcl
                                P.op("dve", lambda e, dc=dc, sub=sub, dcl=dcl: e.tensor_tensor(
                                    out=R[:, dc, sub * 512:(sub + 1) * 512], in0=R[:, dc, sub * 512:(sub + 1) * 512],
                                    in1=pp[dcl][sub][:, :], op=ALU.add), r=[R, pp[dcl][sub]], w=[R])
                    P.dma("sp", res.ap()[:, :, t0:t0 + 1024].rearrange("c p t -> p c t"), R[:, :, :], "d_C_R", r=[R], w=[D_res[tt2]])
                    if debug and last:
                        P.dma("sp", dbg_mix.ap()[:, :, t0:t0 + 1024].rearrange("c p t -> p c t"), R[:, :, :], "d_C_R", r=[R])
                    for c in range(16):
                        S_ = sq.next()
                        P.op("act", lambda e, S_=S_, c=c: e.activation(out=S_[:, :], in_=R[:, c, :], func=AF.Square), r=[R], w=[S_])
                        for sub in range(2):
                            P.op("pe", lambda e, S_=S_, c=c, sub=sub: e.matmul(
                                pn[sub][:, :], lhsT=ones_f[:, :], rhs=S_[:, sub * 512:(sub + 1) * 512], start=(c == 0), stop=(c == 15)),
                                r=[S_, ones_f], w=[pn[sub]])
                    for sub in range(2):
                        P.op("act", lambda e, sub=sub: e.activation(out=rs[:, sub * 512:(sub + 1) * 512], in_=pn[sub][:, :], func=AF.Sqrt,
                                                                     bias=EPS_AP[:, 0:1], scale=1.0 / 2048), r=[pn[sub], EPS_T], w=[rs])
                    P.op("dve", lambda e: e.reciprocal(out=rs[:, :], in_=rs[:, :]), r=[rs], w=[rs])

                    def nrm(e):
                        for c in range(16):
                            last_ = e.scalar_tensor_tensor(out=hn2[:, c, :], in0=R[:, c, :], scalar=mlpg[:, layer * 16 + c:layer * 16 + c + 1],
                                                           in1=ri[:, :], op0=ALU.mult, op1=ALU.mult)
                        return last_
                    P.op("dve", nrm, r=[R, ri, mlpg], w=[hn2])
                    P.end_phase()
                with ExitStack() as st:
                    h1 = P.sb(st, "h1", [128, 64, 1024], BF16)
                    wsl = Rot([P.sb(st, f"M_w{i}", [128, 16, 256], BF16) for i in range(3)])
                    rl = Rot([P.sb(st, f"M_rl{i}", [128, 512], F32) for i in range(2)])
                    rc = Rot([P.sb(st, f"M_rc{i}", [128, 512], F32) for i in range(3)])
                    psr = Rot([P.ps(st, f"M_p{i}", [128, 512], F32) for i in range(8)])
                    W1 = w1.ap()[layer]
                    W2 = w2.ap()[layer]
                    for fp in range(32):
                        W = wsl.next()
                        load_w(W, W1, 0, fp * 256, 256)
                        for fl in range(2):
                            for sub in range(2):
                                p_ = psr.next()

                                def mm1(e, p_=p_, W=W, fl=fl, sub=sub):
                                    for dc in range(16):
                                        last_ = e.matmul(p_[:, :], lhsT=W[:, dc, fl * 128:(fl + 1) * 128],
                                                         rhs=hn2[:, dc, sub * 512:(sub + 1) * 512], start=(dc == 0), stop=(dc == 15))
                                    return last_
                                P.op("pe", mm1, r=[hn2, W], w=[p_])
                                RL = rl.next()
                                P.op("act", lambda e, p_=p_, RL=RL: e.activation(out=RL[:, :], in_=p_[:, :], func=AF.Relu), r=[p_], w=[RL])
                                P.op("dve", lambda e, RL=RL, fp=fp, fl=fl, sub=sub: e.tensor_tensor(
                                    out=h1[:, fp * 2 + fl, sub * 512:(sub + 1) * 512], in0=RL[:, :], in1=RL[:, :], op=ALU.mult), r=[RL], w=[h1])
                    for dcp in range(8):
                        pb4 = [[psr.next() for _ in range(2)] for _ in range(2)]
                        for kq in range(4):
                            W = wsl.next()
                            load_w(W, W2, kq * 16, dcp * 256, 256)

                            def mm2(e, kq=kq, W=W, pb4=pb4):
                                for dcl in range(2):
                                    for sub in range(2):
                                        for c16 in range(16):
                                            last_ = e.matmul(pb4[dcl][sub][:, :], lhsT=W[:, c16, dcl * 128:(dcl + 1) * 128],
                                                             rhs=h1[:, kq * 16 + c16, sub * 512:(sub + 1) * 512],
                                                             start=(kq == 0 and c16 == 0), stop=(kq == 3 and c16 == 15))
                                return last_
                            P.op("pe", mm2, r=[h1, W], w=[pb4[0][0], pb4[0][1], pb4[1][0], pb4[1][1]])
                        for dcl in range(2):
                            for sub in range(2):
                                p_ = pb4[dcl][sub]
                                dc = dcp * 2 + dcl
                                RC = rc.next()
                                tsl = slice(t0 + sub * 512, t0 + (sub + 1) * 512)
                                P.dma("sp", RC[:, :], res.ap()[dc, :, tsl], "d_" + RC.name, r=[D_res[tt2]], w=[RC])
                                P.op("dve", lambda e, RC=RC, p_=p_: e.tensor_tensor(out=RC[:, :], in0=RC[:, :], in1=p_[:, :], op=ALU.add),
                                     r=[RC, p_], w=[RC])
                                P.dma("sp", dst.ap()[dc, :, tsl], RC[:, :], "d_" + RC.name, r=[RC], w=[D_out if last else D_res[tt2]])
                    P.end_phase()

        EPS_T = P.sb(gs, "eps_t", [128, 2], F32)
        EPS_AP = EPS_T
        P.op("pool", lambda e: e.memset(EPS_T[:, 0:1], EPS), w=[EPS_T])
        P.op("pool", lambda e: e.memset(EPS_T[:, 1:2], 1.0), w=[EPS_T])

        if mode == "ssm":
            phase_B_ssm(1)
            return nc
        for layer in range(n_layers):
            phase_A(layer)
            if layer % 2 == 0:
                phase_B_attn(layer)
            else:
                phase_B_ssm(layer)
            for tt2 in range(2):
                phase_C(layer, tt2)
    return nc


def _prep_inputs(inp):
    f = lambda a: np.ascontiguousarray(np.asarray(a, dtype=np.float32))
    x = f(inp["x"])
    pos = np.arange(4096, dtype=np.float32)[:, None]
    inv = (np.float32(500000.0) ** (-np.arange(0, 32, 2, dtype=np.float32) / np.float32(32))).astype(np.float32)
    ang = (pos * inv[None, :]).astype(np.float32)
    tab = lambda a: f(a.reshape(32, 128, 16).transpose(1, 0, 2).reshape(128, 512))
    cos, sin = tab(np.cos(ang)), tab(np.sin(ang))
    gl = lambda a: f(a.reshape(4, 16, 128).transpose(2, 0, 1).reshape(128, 64))
    mixg, mlpg = gl(f(inp["mixer_norm"])), gl(f(inp["mlp_norm"]))
    wqkv_full, win_full = f(inp["attn_w_qkv"]), f(inp["ssm_w_in"])
    convw_full, convb_full = f(inp["ssm_conv_w"]), f(inp["ssm_conv_b"])
    shared = {
        "mixg": mixg, "mlpg": mlpg, "qg": f(inp["attn_q_norm"]), "kg": f(inp["attn_k_norm"]),
        "lam": f(inp["attn_lambda"]).reshape(2, 512), "subln": f(inp["attn_subln"]), "wo": f(inp["attn_w_o"]),
        "wout": f(inp["ssm_w_out"]), "w1": f(inp["mlp_w1"]), "w2": f(inp["mlp_w2"]), "cos": cos, "sin": sin,
    }
    per_rank = []
    for h in range(2):
        cols = []
        for hd in range(4 * h, 4 * h + 4):
            cols += [np.arange(hd * 256, (hd + 1) * 256), 2048 + np.arange(hd * 256, (hd + 1) * 256),
                     4096 + np.arange(hd * 256, (hd + 1) * 256)]
        cols = np.concatenate(cols)
        xs = 4096 + np.arange(h * 2048, (h + 1) * 2048)
        Bs = 8192 + np.arange(h * 512, (h + 1) * 512)
        Cs = 8192 + 1024 + np.arange(h * 512, (h + 1) * 512)
        zs = np.arange(h * 2048, (h + 1) * 2048)
        dts = 10240 + np.arange(h * 32, (h + 1) * 32)
        icol = np.concatenate([xs, Bs, Cs, zs, dts, np.zeros(224, np.int64)])
        cch = np.concatenate([xs, Bs, Cs]) - 4096
        cw = convw_full[:, :, cch]
        cw = f(cw.transpose(0, 2, 1).reshape(2, 24, 128, 4).transpose(0, 2, 1, 3).reshape(2, 128, 96))
        cb = f(convb_full[:, cch].reshape(2, 24, 128).transpose(0, 2, 1))
        hs = slice(h * 32, (h + 1) * 32)
        sel = np.zeros((128, 2), np.float32)
        sel[:, 0] = h
        sel[:, 1] = 1 - h
        per_rank.append({
            "wqkv": f(wqkv_full[:, :, cols]), "win": f(win_full[:, :, icol]), "convw": cw, "convb": cb,
            "dtb": f(inp["ssm_dt_bias"])[:, hs].copy(), "alog": f(inp["ssm_a_log"])[:, hs].copy(), "dsk": f(inp["ssm_d"])[:, hs].copy(),
            "ngain": f(inp["ssm_norm"])[:, h * 2048:(h + 1) * 2048].copy(), "selv": sel,
        })
    maps = []
    for c in range(8):
        b, h = c // 2, c % 2
        m = dict(shared)
        m.update(per_rank[h])
        m["xT"] = f(x[b, h * 2048:(h + 1) * 2048, :].T.reshape(16, 128, 2048))
        maps.append(m)
    return maps


_NC_CACHE = {}


def kernel(**inputs):
    maps = _prep_inputs(inputs)
    if "nc" not in _NC_CACHE:
        _NC_CACHE["nc"] = build(4)
    res = run_bass_kernel_spmd(_NC_CACHE["nc"], maps, core_ids=list(range(8)))
    out = np.empty((4, 4096, 2048), np.float32)
    for c in range(8):
        b, h = c // 2, c % 2
        o = np.asarray(res.results[c]["outT"]).reshape(2048, 2048)
        out[b, h * 2048:(h + 1) * 2048, :] = o.T
    return out
```
